# Optimizing a Trainium2 kernel written in Bass

```python
import math
import jax, jax.numpy as jnp
from jax import lax
import numpy as np

D_MODEL = 1024
BATCH = 8
SEQ = 2048
DEPTH = 2
DEC_BATCH = 128
DEC_SEQ = 4
PAST_LEN = 2048
PAGE_SIZE = 128

GDN_HEADS = 4
GDN_HEAD_DIM = 128
GDN_WIDTH = GDN_HEADS * GDN_HEAD_DIM
GDN_CONV = 4
GDN_CHUNK = 64
SC_WIDTH = 256
SC_CONV = 3
MOBA_HEADS = 4
MOBA_HEAD_DIM = 64
MOBA_WIDTH = MOBA_HEADS * MOBA_HEAD_DIM
MOBA_BLOCK = 256
MOBA_TOPK = 3
MOBA_QCHUNK = 64
MIX_WIDTH = GDN_WIDTH + SC_WIDTH + MOBA_WIDTH
PROJ_WIDTH = 4 * GDN_WIDTH + 2 * GDN_HEADS + 3 * SC_WIDTH + 3 * MOBA_WIDTH
D_FF = 2816
NORM_EPS = 1e-6
NEG_INF = -1e30

kernel_name = 'hymba_gdn_shortconv_moba_macaron_step'


def rms_norm(x, w):
    xf = x.astype(jnp.float32)
    y = xf * lax.rsqrt(jnp.mean(xf * xf, axis=-1, keepdims=True) + NORM_EPS)
    return (y * w.astype(jnp.float32)).astype(x.dtype)


def swiglu(x, w_gate, w_up, w_down):
    return (jax.nn.silu(x @ w_gate) * (x @ w_up)) @ w_down


def causal_conv(x, hist, w):
    width = w.shape[0]
    seq_len = x.shape[1]
    xp = jnp.concatenate([hist.astype(x.dtype), x], axis=1)
    y = xp[:, 0:seq_len] * w[0]
    for i in range(1, width):
        y = y + xp[:, i:i + seq_len] * w[i]
    return y, xp[:, xp.shape[1] - (width - 1):]


def l2_normalize(x):
    return x * lax.rsqrt(jnp.sum(x * x, axis=-1, keepdims=True) + NORM_EPS)


def split_projection(h):
    sizes = (GDN_WIDTH, GDN_WIDTH, GDN_WIDTH, GDN_WIDTH, GDN_HEADS, GDN_HEADS,
             SC_WIDTH, SC_WIDTH, SC_WIDTH, MOBA_WIDTH, MOBA_WIDTH, MOBA_WIDTH)
    outs = []
    off = 0
    for s in sizes:
        outs.append(h[..., off:off + s])
        off += s
    return outs


def gated_delta_rule(q, k, v, beta, g, s0):
    bsz, seq_len, n_heads, dk = q.shape
    cs = min(GDN_CHUNK, seq_len)
    n_chunks = -(-seq_len // cs)
    pad = n_chunks * cs - seq_len
    if pad:
        pw = ((0, 0), (0, pad), (0, 0), (0, 0))
        q, k, v = jnp.pad(q, pw), jnp.pad(k, pw), jnp.pad(v, pw)
        beta, g = jnp.pad(beta, pw[:3]), jnp.pad(g, pw[:3])

    def chunks(t):
        return t.reshape(bsz, n_chunks, cs, n_heads, -1).transpose(1, 0, 3, 2, 4)

    qc, kc, vc = chunks(q), chunks(k), chunks(v)
    bc = chunks(beta[..., None])[..., 0]
    gc = jnp.cumsum(chunks(g[..., None])[..., 0], axis=-1)
    idx = jnp.arange(cs)
    incl = idx[:, None] >= idx[None, :]
    strict = idx[:, None] > idx[None, :]
    diff = gc[..., :, None] - gc[..., None, :]
    decay = jnp.where(incl, jnp.exp(jnp.where(incl, diff, 0.0)), 0.0)
    kk = jnp.einsum('nbhid,nbhjd->nbhij', kc, kc)
    lower = jnp.where(strict, bc[..., :, None] * kk * decay, 0.0)
    unit_lower = lower + jnp.eye(cs, dtype=lower.dtype)
    rhs = jnp.concatenate([kc * (bc * jnp.exp(gc))[..., None], vc * bc[..., None]], axis=-1)
    sol = lax.linalg.triangular_solve(unit_lower, rhs, left_side=True, lower=True,
                                      unit_diagonal=True)
    w_k, u_v = sol[..., :dk], sol[..., dk:]
    attn = jnp.einsum('nbhid,nbhjd->nbhij', qc, kc) * decay
    q_decayed = qc * jnp.exp(gc)[..., None]
    k_tail = kc * jnp.exp(gc[..., -1:] - gc)[..., None]
    chunk_decay = jnp.exp(gc[..., -1])[..., None, None]

    def step(s, xs):
        w_i, u_i, a_i, qd_i, kt_i, cd_i = xs
        u = u_i - jnp.einsum('bhid,bhde->bhie', w_i, s)
        o = jnp.einsum('bhid,bhde->bhie', qd_i, s) + jnp.einsum('bhij,bhje->bhie', a_i, u)
        s = s * cd_i + jnp.einsum('bhid,bhie->bhde', kt_i, u)
        return s, o

    s_fin, o = lax.scan(step, s0, (w_k, u_v, attn, q_decayed, k_tail, chunk_decay))
    o = o.transpose(1, 0, 3, 2, 4).reshape(bsz, n_chunks * cs, n_heads, -1)[:, :seq_len]
    return o, s_fin


def gdn_mixer(q_in, k_in, v_in, z, b, a, conv_hist, s0, conv_w, a_log, dt_bias, norm_w):
    bsz, seq_len, _ = q_in.shape
    qkv, hist_new = causal_conv(jnp.concatenate([q_in, k_in, v_in], axis=-1), conv_hist, conv_w)
    qkv = jax.nn.silu(qkv.astype(jnp.float32))
    q, k, v = jnp.split(qkv, 3, axis=-1)
    shp = (bsz, seq_len, GDN_HEADS, GDN_HEAD_DIM)
    q = l2_normalize(q.reshape(shp)) * (GDN_HEAD_DIM ** -0.5)
    k = l2_normalize(k.reshape(shp))
    v = v.reshape(shp)
    beta = jax.nn.sigmoid(b.astype(jnp.float32))
    g = -jnp.exp(a_log.astype(jnp.float32)) * jax.nn.softplus(
        a.astype(jnp.float32) + dt_bias.astype(jnp.float32))
    o, s_new = gated_delta_rule(q, k, v, beta, g, s0.astype(jnp.float32))
    o = rms_norm(o, norm_w) * jax.nn.silu(z.reshape(shp).astype(jnp.float32))
    return o.reshape(bsz, seq_len, GDN_WIDTH).astype(q_in.dtype), hist_new, s_new.astype(s0.dtype)


def short_conv_mixer(h, c_gate, b_gate, hist, conv_w):
    y, hist_new = causal_conv(c_gate * h, hist, conv_w)
    return b_gate * y, hist_new


def moba_attention(q, k_full, v_full, q_pos0):
    bsz, q_len, n_heads, hd = q.shape
    k_len = k_full.shape[1]
    qf = q.astype(jnp.float32).transpose(0, 2, 1, 3) * (hd ** -0.5)
    kf = k_full.astype(jnp.float32)
    vf = v_full.astype(jnp.float32)
    n_blocks = max(-(-k_len // MOBA_BLOCK), MOBA_TOPK)
    tail = n_blocks * MOBA_BLOCK - k_len

    def blocks(t):
        t = jnp.pad(t, ((0, 0), (0, tail), (0, 0), (0, 0)))
        return t.reshape(bsz, n_blocks, MOBA_BLOCK, n_heads, hd).transpose(0, 3, 1, 2, 4)

    kb, vb = blocks(kf), blocks(vf)
    k_mean = jnp.mean(kb, axis=3)

    def window_src(t):
        return jnp.pad(t, ((0, 0), (MOBA_BLOCK - 1, 0), (0, 0), (0, 0))).transpose(0, 2, 1, 3)

    kw_src, vw_src = window_src(kf), window_src(vf)
    qc = MOBA_QCHUNK if q_len % MOBA_QCHUNK == 0 else q_len
    n_qc = q_len // qc
    win = MOBA_BLOCK + qc - 1
    q_chunks = qf.reshape(bsz, n_heads, n_qc, qc, hd).transpose(2, 0, 1, 3, 4)
    starts = q_pos0 + jnp.arange(n_qc, dtype=jnp.int32) * qc
    b_idx = jnp.arange(bsz)[:, None, None, None]
    h_idx = jnp.arange(n_heads)[None, :, None, None]
    blk_ids = jnp.arange(n_blocks)
    rank = jnp.arange(MOBA_TOPK)

    def one_chunk(args):
        qi, s = args
        t = s + jnp.arange(qc)
        bt = t // MOBA_BLOCK
        gate = jnp.einsum('bhqd,bhnd->bhqn', qi, k_mean)
        gate = jnp.where(blk_ids[None, :] < bt[:, None], gate, NEG_INF)
        _, sel = lax.top_k(gate, MOBA_TOPK)
        valid = rank[None, :] < bt[:, None]
        ks = kb[b_idx, h_idx, sel]
        vs = vb[b_idx, h_idx, sel]
        s_sel = jnp.einsum('bhqd,bhqnkd->bhqnk', qi, ks)
        s_sel = jnp.where(valid[:, :, None], s_sel, NEG_INF)
        s_sel = s_sel.reshape(bsz, n_heads, qc, MOBA_TOPK * MOBA_BLOCK)
        kw = lax.dynamic_slice_in_dim(kw_src, s, win, axis=2)
        vw = lax.dynamic_slice_in_dim(vw_src, s, win, axis=2)
        pos = s - (MOBA_BLOCK - 1) + jnp.arange(win)
        own = (pos[None, :] >= (bt * MOBA_BLOCK)[:, None]) & (pos[None, :] <= t[:, None])
        s_own = jnp.where(own, jnp.einsum('bhqd,bhkd->bhqk', qi, kw), NEG_INF)
        p = jax.nn.softmax(jnp.concatenate([s_sel, s_own], axis=-1), axis=-1)
        p_sel = p[..., :MOBA_TOPK * MOBA_BLOCK].reshape(bsz, n_heads, qc, MOBA_TOPK, MOBA_BLOCK)
        p_own = p[..., MOBA_TOPK * MOBA_BLOCK:]
        return (jnp.einsum('bhqnk,bhqnkd->bhqd', p_sel, vs)
                + jnp.einsum('bhqk,bhkd->bhqd', p_own, vw))

    out = lax.map(one_chunk, (q_chunks, starts))
    return out.transpose(1, 0, 3, 2, 4).reshape(bsz, q_len, n_heads, hd).astype(q.dtype)


def decoder_layer(x, gdn_s0, gdn_hist, sc_hist, k_past, v_past,
                  n_ffn1, ffn1_w_gate, ffn1_w_up, ffn1_w_down, n_mix, w_in,
                  gdn_conv_w, gdn_a_log, gdn_dt_bias, gdn_norm_w, sc_conv_w, w_out,
                  n_ffn2, ffn2_w_gate, ffn2_w_up, ffn2_w_down):
    bsz, seq_len, _ = x.shape
    x = x + 0.5 * swiglu(rms_norm(x, n_ffn1), ffn1_w_gate, ffn1_w_up, ffn1_w_down)
    h = rms_norm(x, n_mix) @ w_in
    q_a, k_a, v_a, z_a, b_a, a_a, h_b, c_b, g_b, q_c, k_c, v_c = split_projection(h)
    o_a, gdn_hist_new, gdn_s_new = gdn_mixer(q_a, k_a, v_a, z_a, b_a, a_a, gdn_hist, gdn_s0,
                                             gdn_conv_w, gdn_a_log, gdn_dt_bias, gdn_norm_w)
    o_b, sc_hist_new = short_conv_mixer(h_b, c_b, g_b, sc_hist, sc_conv_w)
    mshp = (bsz, seq_len, MOBA_HEADS, MOBA_HEAD_DIM)
    q_c, k_c, v_c = q_c.reshape(mshp), k_c.reshape(mshp), v_c.reshape(mshp)
    k_full = jnp.concatenate([k_past.astype(x.dtype), k_c], axis=1)
    v_full = jnp.concatenate([v_past.astype(x.dtype), v_c], axis=1)
    o_c = moba_attention(q_c, k_full, v_full, k_past.shape[1]).reshape(bsz, seq_len, MOBA_WIDTH)
    x = x + jnp.concatenate([o_a, o_b, o_c], axis=-1) @ w_out
    x = x + 0.5 * swiglu(rms_norm(x, n_ffn2), ffn2_w_gate, ffn2_w_up, ffn2_w_down)
    return x, gdn_s_new, gdn_hist_new, sc_hist_new, k_c, v_c


def trunk(x, layer_states, layer_params):
    new_states = []
    for l in range(DEPTH):
        out = decoder_layer(x, *layer_states[l], *[p[l] for p in layer_params])
        x = out[0]
        new_states.append(out[1:])
    stacked = [jnp.stack([st[i] for st in new_states]) for i in range(5)]
    return x, stacked


def setup_inputs(seed: int = 0) -> dict:
    key = jax.random.key(seed)
    ks = jax.random.split(key, 32)
    n_pages = PAST_LEN // PAGE_SIZE
    n_pool = (5 * DEC_BATCH * n_pages) // 4

    def nrm(k, shape, scale):
        return jax.random.normal(k, shape, jnp.float32) * scale

    def gain(k, shape):
        return 1.0 + 0.01 * jax.random.normal(k, shape, jnp.float32)

    dt = jnp.exp(jax.random.uniform(ks[14], (DEPTH, GDN_HEADS), jnp.float32,
                                    minval=math.log(1e-3), maxval=math.log(1e-1)))
    page_table = jax.random.permutation(ks[7], n_pool)[:DEC_BATCH * n_pages]
    page_table = page_table.reshape(DEC_BATCH, n_pages).astype(jnp.int32)
    return {
        'x_prompt': nrm(ks[0], (BATCH, SEQ, D_MODEL), 1.0),
        'x_sample': nrm(ks[1], (DEC_BATCH, DEC_SEQ, D_MODEL), 1.0),
        'state_gdn': nrm(ks[2], (DEPTH, DEC_BATCH, GDN_HEADS, GDN_HEAD_DIM, GDN_HEAD_DIM),
                         GDN_HEAD_DIM ** -0.5),
        'state_gdn_conv': nrm(ks[3], (DEPTH, DEC_BATCH, GDN_CONV - 1, 3 * GDN_WIDTH), 1.0),
        'state_sconv': nrm(ks[4], (DEPTH, DEC_BATCH, SC_CONV - 1, SC_WIDTH), 1.0),
        'cache_k': nrm(ks[5], (DEPTH, n_pool, PAGE_SIZE, MOBA_HEADS, MOBA_HEAD_DIM), 1.0),
        'cache_v': nrm(ks[6], (DEPTH, n_pool, PAGE_SIZE, MOBA_HEADS, MOBA_HEAD_DIM), 1.0),
        'page_table': page_table,
        'norm_ffn1': gain(ks[8], (DEPTH, D_MODEL)),
        'ffn1_w_gate': nrm(ks[9], (DEPTH, D_MODEL, D_FF), D_MODEL ** -0.5),
        'ffn1_w_up': nrm(ks[10], (DEPTH, D_MODEL, D_FF), D_MODEL ** -0.5),
        'ffn1_w_down': nrm(ks[11], (DEPTH, D_FF, D_MODEL), D_FF ** -0.5),
        'norm_mix': gain(ks[12], (DEPTH, D_MODEL)),
        'w_in': nrm(ks[13], (DEPTH, D_MODEL, PROJ_WIDTH), D_MODEL ** -0.5),
        'gdn_conv_w': nrm(ks[15], (DEPTH, GDN_CONV, 3 * GDN_WIDTH), GDN_CONV ** -0.5),
        'gdn_a_log': jnp.log(jax.random.uniform(ks[16], (DEPTH, GDN_HEADS), jnp.float32,
                                                minval=1.0, maxval=16.0)),
        'gdn_dt_bias': dt + jnp.log(-jnp.expm1(-dt)),
        'gdn_norm_w': gain(ks[17], (DEPTH, GDN_HEAD_DIM)),
        'sc_conv_w': nrm(ks[18], (DEPTH, SC_CONV, SC_WIDTH), SC_CONV ** -0.5),
        'w_out': nrm(ks[19], (DEPTH, MIX_WIDTH, D_MODEL), MIX_WIDTH ** -0.5),
        'norm_ffn2': gain(ks[20], (DEPTH, D_MODEL)),
        'ffn2_w_gate': nrm(ks[21], (DEPTH, D_MODEL, D_FF), D_MODEL ** -0.5),
        'ffn2_w_up': nrm(ks[22], (DEPTH, D_MODEL, D_FF), D_MODEL ** -0.5),
        'ffn2_w_down': nrm(ks[23], (DEPTH, D_FF, D_MODEL), D_FF ** -0.5),
        'norm_final': gain(ks[24], (D_MODEL,)),
    }


def reference(x_prompt, x_sample, state_gdn, state_gdn_conv, state_sconv, cache_k, cache_v,
              page_table, norm_ffn1, ffn1_w_gate, ffn1_w_up, ffn1_w_down, norm_mix, w_in,
              gdn_conv_w, gdn_a_log, gdn_dt_bias, gdn_norm_w, sc_conv_w, w_out,
              norm_ffn2, ffn2_w_gate, ffn2_w_up, ffn2_w_down, norm_final):
    layer_params = (norm_ffn1, ffn1_w_gate, ffn1_w_up, ffn1_w_down, norm_mix, w_in,
                    gdn_conv_w, gdn_a_log, gdn_dt_bias, gdn_norm_w, sc_conv_w, w_out,
                    norm_ffn2, ffn2_w_gate, ffn2_w_up, ffn2_w_down)
    pb = x_prompt.shape[0]
    pdt = x_prompt.dtype
    prompt_states = [(jnp.zeros((pb, GDN_HEADS, GDN_HEAD_DIM, GDN_HEAD_DIM), pdt),
                      jnp.zeros((pb, GDN_CONV - 1, 3 * GDN_WIDTH), pdt),
                      jnp.zeros((pb, SC_CONV - 1, SC_WIDTH), pdt),
                      jnp.zeros((pb, 0, MOBA_HEADS, MOBA_HEAD_DIM), pdt),
                      jnp.zeros((pb, 0, MOBA_HEADS, MOBA_HEAD_DIM), pdt)) for _ in range(DEPTH)]
    db, n_seq_pages = page_table.shape
    past_shape = (db, n_seq_pages * PAGE_SIZE, MOBA_HEADS, MOBA_HEAD_DIM)
    sample_states = [(state_gdn[l], state_gdn_conv[l], state_sconv[l],
                      cache_k[l][page_table].reshape(past_shape),
                      cache_v[l][page_table].reshape(past_shape)) for l in range(DEPTH)]
    h_p, (p_gdn, p_conv, p_sconv, p_k, p_v) = trunk(x_prompt, prompt_states, layer_params)
    h_s, (s_gdn, s_conv, s_sconv, s_k, s_v) = trunk(x_sample, sample_states, layer_params)
    y_prompt = rms_norm(h_p, norm_final)
    y_sample = rms_norm(h_s, norm_final)
    return (y_prompt, y_sample, p_gdn, p_conv, p_sconv, p_k, p_v,
            s_gdn, s_conv, s_sconv, s_k, s_v)
```

```python
import contextlib
import numpy as np
import concourse.bass as bass
import concourse.mybir as mybir
from concourse.bass_utils import run_bass_kernel_spmd

F32 = mybir.dt.float32
BF16 = mybir.dt.bfloat16
I32 = mybir.dt.int32
AF = mybir.ActivationFunctionType
ALU = mybir.AluOpType
AX = mybir.AxisListType

NCORES = 8
D = 1024
KC = 8
DFF = 2816
PW = 3592
TP = 2048
NSQ = 16
TS = 64
T = TP + TS
DEPTH = 2
EPS = 1e-6
GROUPS = [(0, 512), (512, 512), (1024, 512), (1536, 512), (2048, 64)]
PASSES = [(0, 6), (6, 6), (12, 5), (17, 5)]


class _Op:
    __slots__ = ("eng", "fn", "is_dma", "deps", "signal", "tick", "dsem", "dval")

    def __init__(self, eng, fn, is_dma):
        self.eng = eng
        self.fn = fn
        self.is_dma = is_dma
        self.deps = []
        self.signal = False
        self.tick = 0
        self.dsem = -1
        self.dval = 0


class Sched:
    def __init__(self, nc, n_dma_sems=64):
        self.nc = nc
        self.ops = []
        self.last_w = {}
        self.readers = {}
        self.n_dma_sems = n_dma_sems
        self.dma_count = 0
        self.sw_count = 0
        self.hw_count = 0
        self.dma_last = [None] * n_dma_sems
        self.dma_cnt_per = [0] * n_dma_sems
        self.last_on = {}
        self.engs = {"pe": nc.tensor, "act": nc.scalar, "dve": nc.vector, "pool": nc.gpsimd,
                     "sp": nc.sync}

    def _add_dep(self, op, tgt, raw):
        if tgt is None or tgt is op:
            return
        if not tgt.is_dma and not op.is_dma and tgt.eng == op.eng and op.eng == "pe":
            return
        op.deps.append(tgt)

    def _record(self, op, reads, writes):
        pr = [k for k in reads if k.startswith("ps")]
        if pr:
            writes = list(writes) + [k for k in pr if k not in writes]
        for k in reads:
            self._add_dep(op, self.last_w.get(k), True)
        for k in writes:
            self._add_dep(op, self.last_w.get(k), False)
            for r in self.readers.get(k, ()):
                self._add_dep(op, r, False)
        for k in reads:
            self.readers.setdefault(k, []).append(op)
        for k in writes:
            self.last_w[k] = op
            self.readers[k] = []
        self.ops.append(op)
        if not op.is_dma:
            self.last_on[op.eng] = op
        return op

    def op(self, eng, fn, reads=(), writes=()):
        return self._record(_Op(eng, fn, False), reads, writes)

    def dma(self, queue, fn, reads=(), writes=()):
        op = _Op(queue, fn, True)
        nsw = self.n_dma_sems // 3
        if queue == "pool":
            k = self.sw_count % nsw
            self.sw_count += 1
        else:
            k = nsw + self.hw_count % (self.n_dma_sems - nsw)
            self.hw_count += 1
        self.dma_count += 1
        op.dsem = k
        self.dma_cnt_per[k] += 1
        op.dval = 16 * self.dma_cnt_per[k]
        prev = self.dma_last[k]
        if prev is not None:
            op.deps.append(prev)
        self.dma_last[k] = op
        return self._record(op, reads, writes)

    def barrier(self):
        tg = [o for o in self.last_on.values()] + [d for d in self.dma_last if d is not None]
        for e in self.engs:
            b = _Op(e, None, False)
            b.deps = [t for t in tg]
            self.ops.append(b)
        self.last_w = {}
        self.readers = {}

    def emit(self, final_wait_eng="sp"):
        nc = self.nc
        fin = _Op(final_wait_eng, None, False)
        fin.deps = [d for d in self.dma_last if d is not None]
        self.ops.append(fin)
        for o in self.ops:
            for t in o.deps:
                if not t.is_dma:
                    t.signal = True
        cnt = {e: 0 for e in self.engs}
        for o in self.ops:
            if not o.is_dma and o.signal:
                cnt[o.eng] += 1
                o.tick = cnt[o.eng]
        with contextlib.ExitStack() as es:
            esem = {e: es.enter_context(nc.semaphore(f"s_{e}")) for e in self.engs}
            dsem = [es.enter_context(nc.semaphore(f"d_{i}")) for i in range(self.n_dma_sems)]
            waited = {e: {} for e in self.engs}
            n_wait = 0
            for o in self.ops:
                eng = self.engs[o.eng]
                w = waited[o.eng]
                need = {}
                for t in o.deps:
                    if t.is_dma:
                        key, val = ("d", t.dsem), t.dval
                    else:
                        key, val = ("e", t.eng), t.tick
                    if w.get(key, 0) >= val:
                        continue
                    if need.get(key, 0) < val:
                        need[key] = val
                for key, val in need.items():
                    sem = dsem[key[1]] if key[0] == "d" else esem[key[1]]
                    eng.wait_ge(sem, val)
                    w[key] = val
                    n_wait += 1
                if o.fn is None:
                    continue
                ins = o.fn(eng)
                if o.is_dma:
                    ins.then_inc(dsem[o.dsem], 16)
                elif o.signal:
                    ins.then_inc(esem[o.eng], 1)
            self.stats = dict(n_ops=len(self.ops), n_wait=n_wait, ticks=cnt, n_dma=self.dma_count)


def build_nc(stage=99):
    nc = bass.Bass("TRN2", target_bir_lowering=False)

    def din(name, shape, dt=F32):
        return nc.dram_tensor(name, list(shape), dt, kind="ExternalInput").ap()

    def dout(name, shape, dt=F32):
        return nc.dram_tensor(name, list(shape), dt, kind="ExternalOutput").ap()

    xin = din("xin", [T, D])
    norms = din("norms", [7, D])
    wg = [din("ffn1_w_gate", [DEPTH, D, DFF]), din("ffn2_w_gate", [DEPTH, D, DFF])]
    wu = [din("ffn1_w_up", [DEPTH, D, DFF]), din("ffn2_w_up", [DEPTH, D, DFF])]
    wd = [din("ffn1_w_down", [DEPTH, DFF, D]), din("ffn2_w_down", [DEPTH, DFF, D])]
    y = dout("y", [T, D])
    w_in = din("w_in", [DEPTH, D, PW])
    w_out = din("w_out", [DEPTH, D, D])
    sc_conv_w = din("sc_conv_w", [DEPTH, 3, 256])
    state_sconv = din("state_sconv", [DEPTH, NSQ, 2, 256])
    p_sconv = dout("p_sconv", [DEPTH, 2, 256])
    p_conv = dout("p_conv", [DEPTH, 3, 1536])
    s_conv = dout("s_conv", [DEPTH, NSQ, 3, 1536])
    p_k = dout("p_k", [DEPTH, TP, 256])
    p_v = dout("p_v", [DEPTH, TP, 256])
    s_k = dout("s_k", [DEPTH, NSQ, 4, 256])
    s_v = dout("s_v", [DEPTH, NSQ, 4, 256])
    p_gdn = dout("p_gdn", [DEPTH, 4, 128, 128])
    state_gdn = din("state_gdn", [DEPTH, NSQ, 4, 128, 128])
    cache_k = din("cache_k", [DEPTH, 2560, 128, 4, 64])
    cache_v = din("cache_v", [DEPTH, 2560, 128, 4, 64])
    page_table = din("page_table", [1, NSQ * 16], I32)
    state_gdn_conv = din("state_gdn_conv", [DEPTH, NSQ, 3, 1536])
    gdn_conv_w = din("gdn_conv_w", [DEPTH, 4, 1536])
    gdn_a_log = din("gdn_a_log", [DEPTH, 4])
    gdn_dt_bias = din("gdn_dt_bias", [DEPTH, 4])
    gdn_norm_w = din("gdn_norm_w", [DEPTH, 128])
    s_gdn = dout("s_gdn", [DEPTH, NSQ, 4, 128, 128])
    s_sconv = dout("s_sconv", [DEPTH, NSQ, 2, 256])

    with contextlib.ExitStack() as es:
        def sb(name, shape, dt=F32):
            return es.enter_context(nc.sbuf_tensor(name, list(shape), dt))

        def ps(name, shape, dt=F32):
            return es.enter_context(nc.psum_tensor(name, list(shape), dt))

        S = Sched(nc)
        xT = sb("xT", [128, KC, T])
        xnT = sb("xnT", [128, KC, T], BF16)
        ident = sb("ident", [128, 128])
        ones = sb("ones", [128, 128])
        epsb = sb("epsb", [128, 1])
        gains = sb("gains", [128, 7, KC])
        arena = sb("arena", [128, 18944])

        def av(off, nw, dt=F32, pat=None, **kw):
            v = arena[:, off:off + nw]
            if dt != F32:
                v = v.bitcast(dt)
            if pat is not None:
                v = v.rearrange(pat, **kw)
            return v
        hT = av(0, 6336, BF16, "p (a t) -> p a t", a=6)
        wgu = [av(6336 + 1024 * i, 1024, BF16, "p (j k f) -> p j k f", j=2, k=KC) for i in range(2)]
        wdb = [av(8384 + 3072 * i, 3072, BF16, "p (c n) -> p c n", c=6) for i in range(2)]
        mixT = av(0, 8448, BF16, "p (a t) -> p a t", a=8)
        WB = 8448
        wcol = [av(WB + 512 * i, 512, BF16, "p (k f) -> p k f", k=KC) for i in range(2)]
        pbuf = {n: av(WB + 1024 + 2112 * i, 2112) for i, n in enumerate("ABCD")}
        tmpw = av(WB + 9472, 1024)
        woutb = av(WB + 1024, 4096, BF16, "p (k n) -> p k n", k=KC)
        Vb = av(WB + 1024 + 3 * 2112, 2048, BF16, "p (t f) -> p t f", t=16)
        scw = sb("scw", [128, DEPTH, 2, 3])
        onecol = sb("onecol", [128, 1])
        sg = [sb(f"sg{i}", [128, 512]) for i in range(2)]
        sq = [sb(f"sq{i}", [128, 512]) for i in range(2)]
        acc = sb("acc", [128, 512])
        rstd = sb("rstd", [128, 512])
        xtok = [sb(f"xtok{i}", [128, D]) for i in range(2)]
        PS = [ps(f"ps{i}", [128, 512]) for i in range(8)]

        S.op("pool", lambda e: e.memset(ident[:], 1.0), writes=["ident"])

        def mk_ident(e):
            return e.affine_select(out=ident[:], in_=ident[:], pattern=[[-1, 128]],
                                   compare_op=ALU.is_equal, fill=0.0, base=0, channel_multiplier=1)
        S.op("pool", mk_ident, reads=["ident"], writes=["ident"])
        S.op("pool", lambda e: e.memset(ones[:], 1.0), writes=["ones"])
        S.op("pool", lambda e: e.memset(epsb[:], EPS), writes=["epsb"])
        S.op("pool", lambda e: e.memset(onecol[:], 1.0), writes=["onecol"])
        negm = sb("negm", [128, 8, 8])
        pastm = sb("pastm", [128, 8, 8])
        ownm = sb("ownm", [128, 8, 8])
        E8 = sb("E8", [8, 9, 128])
        triuf = sb("triuf", [128, 128])
        onesb = sb("onesb", [128, 128], BF16)
        kmT = sb("kmT", [128, 8])

        def mk_sel(t, inval, op, fill):
            def f(e):
                return e.affine_select(out=t[:], in_=t[:], pattern=[[1, 8], [-1, 8]], compare_op=op, fill=fill,
                                       base=0, channel_multiplier=0)
            return f
        S.op("pool", lambda e: e.memset(negm[:], 0.0), reads=["negm"], writes=["negm"])
        S.op("pool", mk_sel(negm, 0.0, ALU.is_gt, -1e30), reads=["negm"], writes=["negm"])
        S.op("pool", lambda e: e.memset(pastm[:], 1.0), writes=["pastm"])
        S.op("pool", mk_sel(pastm, 1.0, ALU.is_gt, 0.0), reads=["pastm"], writes=["pastm"])
        S.op("pool", lambda e: e.memset(ownm[:], 1.0), writes=["ownm"])
        S.op("pool", mk_sel(ownm, 1.0, ALU.is_equal, 0.0), reads=["ownm"], writes=["ownm"])

        S.op("pool", lambda e: e.memset(E8[:], 1.0), writes=["E8"])

        def mk_e8(e):
            return e.affine_select(out=E8[:], in_=E8[:], pattern=[[-1, 9], [0, 128]], compare_op=ALU.is_equal,
                                   fill=0.0, base=0, channel_multiplier=1)
        S.op("pool", mk_e8, reads=["E8"], writes=["E8"])

        S.op("pool", lambda e: e.memset(triuf[:], 1.0), writes=["triuf"])

        def mk_triu(e):
            return e.affine_select(out=triuf[:], in_=triuf[:], pattern=[[1, 128]], compare_op=ALU.is_ge, fill=0.0,
                                   base=0, channel_multiplier=-1)
        S.op("pool", mk_triu, reads=["triuf"], writes=["triuf"])
        S.op("pool", lambda e: e.memset(onesb[:], 1.0), writes=["onesb"])
        mNL = sb("mNL", [128, 128])
        mNU = sb("mNU", [128, 128])
        sL = sb("sL", [128, 128])
        sU = sb("sU", [128, 128])

        def mk_m(t, cm, pat, op, fill):
            return lambda e: e.affine_select(out=t[:], in_=t[:], pattern=[[pat, 128]], compare_op=op, fill=fill,
                                             base=0, channel_multiplier=cm)
        for (t_, nm_, inv_, cm_, pat_, op_, fill_) in ((mNL, "mNL", 0.0, 1, -1, ALU.is_ge, -1e30),
                                                     (mNU, "mNU", 0.0, -1, 1, ALU.is_ge, -1e30),
                                                     (sL, "sL", 1.0, 1, -1, ALU.is_gt, 0.0),
                                                     (sU, "sU", 1.0, -1, 1, ALU.is_gt, 0.0)):
            S.op("pool", lambda e, t_=t_, inv_=inv_: e.memset(t_[:], inv_), writes=[nm_])
            S.op("pool", mk_m(t_, cm_, pat_, op_, fill_), reads=[nm_], writes=[nm_])
        pidx = sb("pidx", [128, NSQ * 16], I32)
        pidx1 = sb("pidx1", [128, NSQ * 16], I32)
        piota = sb("piota", [128, 1], I32)
        bm4 = sb("bm4", [128, 2, 4])
        cm4 = sb("cm4", [4, 4, 4])
        S.dma("sp", lambda e: e.dma_start(out=pidx[:], in_=page_table[0:1, :].partition_broadcast(128)), writes=["pidx"])
        S.op("pool", lambda e: e.iota(out=piota[:], pattern=[[0, 1]], base=0, channel_multiplier=1), writes=["piota"])
        S.op("dve", lambda e: e.tensor_scalar(out=pidx[:], in0=pidx[:], scalar1=128, scalar2=piota[:, 0:1], op0=ALU.mult,
                                              op1=ALU.add), reads=["pidx", "piota"], writes=["pidx"])
        S.op("dve", lambda e: e.tensor_scalar(out=pidx1[:], in0=pidx[:], scalar1=2560 * 128, scalar2=None, op0=ALU.add),
             reads=["pidx"], writes=["pidx1"])
        S.op("pool", lambda e: e.memset(bm4[:], 1.0), writes=["bm4"])
        S.op("pool", lambda e: e.affine_select(out=bm4[:], in_=bm4[:], pattern=[[128, 2], [-64, 4]], compare_op=ALU.is_ge,
                                               fill=0.0, base=0, channel_multiplier=1), reads=["bm4"], writes=["bm4"])
        S.op("pool", lambda e: e.affine_select(out=bm4[:], in_=bm4[:], pattern=[[-128, 2], [64, 4]], compare_op=ALU.is_gt,
                                               fill=0.0, base=64, channel_multiplier=-1), reads=["bm4"], writes=["bm4"])
        S.op("pool", lambda e: e.memset(cm4[:], 1.0), writes=["cm4"])
        S.op("pool", lambda e: e.affine_select(out=cm4[:], in_=cm4[:], pattern=[[0, 4], [1, 4]], compare_op=ALU.is_ge,
                                               fill=0.0, base=0, channel_multiplier=-1), reads=["cm4"], writes=["cm4"])
        gcw = sb("gcw", [128, DEPTH, 12, 4])
        gnw = sb("gnw", [128, DEPTH])
        albt = sb("albt", [128, DEPTH, 4])
        dtbt = sb("dtbt", [128, DEPTH, 4])
        bg = sb("bg", [128, 16, 8])
        bgs = sb("bgs", [4, NSQ, 8])
        Sp = sb("Sp", [128, 128])
        Ssm = av(5280, 512, F32, "p (s v) -> p s v", s=4)
        cs = av(5792, 336, F32, "p (j s t) -> p j s t", j=3, s=NSQ)
        for l_ in range(DEPTH):
            for c_ in range(12):
                S.dma("sp", lambda e, l_=l_, c_=c_: e.dma_start(
                    out=gcw[:, l_, c_, :], in_=gdn_conv_w[l_][:, c_ * 128:(c_ + 1) * 128].rearrange("i p -> p i"),
                    allow_slow_non_contiguous=True), writes=["gcw"])
        S.dma("sp", lambda e: e.dma_start(out=gnw[:], in_=gdn_norm_w.rearrange("l p -> p l"),
                                          allow_slow_non_contiguous=True), writes=["gnw"])
        for l_ in range(DEPTH):
            S.dma("sp", lambda e, l_=l_: e.dma_start(out=albt[:, l_, :], in_=gdn_a_log[l_:l_ + 1, :].partition_broadcast(128)),
                  writes=["albt"])
            S.dma("sp", lambda e, l_=l_: e.dma_start(out=dtbt[:, l_, :], in_=gdn_dt_bias[l_:l_ + 1, :].partition_broadcast(128)),
                  writes=["dtbt"])
        S.op("act", lambda e: e.activation(out=albt[:], in_=albt[:], func=AF.Exp), reads=["albt"], writes=["albt"])
        S.op("dve", lambda e: e.tensor_scalar(out=albt[:], in0=albt[:], scalar1=-1.0, scalar2=None, op0=ALU.mult),
             reads=["albt"], writes=["albt"])
        S.dma("sp", lambda e: e.dma_start(out=gains[:], in_=norms.rearrange("n (k p) -> p n k", p=128),
                                          allow_slow_non_contiguous=True), writes=["gains"])

        for tt in range(17):
            n = 128 if tt < 16 else 64
            b = tt % 2
            S.dma("sp" if tt % 2 == 0 else "act",
                  lambda e, tt=tt, n=n, b=b: e.dma_start(out=xtok[b][0:n, :], in_=xin[tt * 128:tt * 128 + n, :]),
                  writes=[f"xtok{b}"])
            for half in range(2):
                pt = PS[6 + half]

                def tr(e, tt=tt, n=n, b=b, half=half, pt=pt):
                    for j in range(4):
                        k = half * 4 + j
                        r = e.transpose(out=pt[:, j * 128:j * 128 + n], in_=xtok[b][0:n, k * 128:(k + 1) * 128],
                                        identity=ident[0:n, 0:n])
                    return r
                S.op("pe", tr, reads=[f"xtok{b}", "ident"], writes=[f"ps{6 + half}"])
                S.op("dve" if half == 0 else "act",
                     (lambda e, tt=tt, n=n, half=half, pt=pt: e.tensor_copy(
                         out=xT[:, half * 4:half * 4 + 4, tt * 128:tt * 128 + n],
                         in_=pt[:].rearrange("p (j t) -> p j t", j=4)[:, :, 0:n])) if half == 0 else
                     (lambda e, tt=tt, n=n, half=half, pt=pt: e.copy(
                         out=xT[:, half * 4:half * 4 + 4, tt * 128:tt * 128 + n],
                         in_=pt[:].rearrange("p (j t) -> p j t", j=4)[:, :, 0:n])),
                     reads=[f"ps{6 + half}"], writes=[f"xT{tt // 4}"])

        def rms_stats(g, t0, n):
            for k in range(KC):
                if k == 0:
                    S.op("act", lambda e: e.activation(out=acc[:, :n], in_=xT[:, 0, t0:t0 + n], func=AF.Square),
                         reads=[f"xT{g}"], writes=["acc"])
                else:
                    b = k % 2
                    S.op("act", lambda e, k=k, b=b: e.activation(out=sq[b][:, :n], in_=xT[:, k, t0:t0 + n],
                                                                 func=AF.Square),
                         reads=[f"xT{g}"], writes=[f"sq{b}"])
                    S.op("pool", lambda e, b=b: e.tensor_tensor(out=acc[:, :n], in0=acc[:, :n], in1=sq[b][:, :n],
                                                                op=ALU.add),
                         reads=["acc", f"sq{b}"], writes=["acc"])
            S.op("pe", lambda e: e.matmul(PS[5][:, :n], lhsT=ones[:], rhs=acc[:, :n], start=True, stop=True),
                 reads=["ones", "acc"], writes=["ps5"])
            S.op("act", lambda e: e.activation(out=rstd[:, :n], in_=PS[5][:, :n], func=AF.Sqrt,
                                               bias=epsb[:, 0:1], scale=1.0 / D),
                 reads=["ps5", "epsb"], writes=["rstd"])
            S.op("dve", lambda e: e.reciprocal(out=rstd[:, :n], in_=rstd[:, :n]), reads=["rstd"], writes=["rstd"])

        def rms_norm_all(gi):
            for g, (t0, n) in enumerate(GROUPS):
                rms_stats(g, t0, n)
                for k in range(KC):
                    S.op("dve", lambda e, k=k, t0=t0, n=n: e.scalar_tensor_tensor(
                        out=xnT[:, k, t0:t0 + n], in0=xT[:, k, t0:t0 + n], scalar=gains[:, gi, k:k + 1],
                        in1=rstd[:, :n], op0=ALU.mult, op1=ALU.mult),
                        reads=[f"xT{g}", "rstd", "gains"], writes=[f"xn{g}"])

        wcount = [0]

        def ffn(l, which, gi):
            rms_norm_all(gi)
            WG, WU, WD = wg[which][l], wu[which][l], wd[which][l]
            for pi, (c0, ncp) in enumerate(PASSES):
                wb = wdb[pi % 2]
                S.dma("pool", lambda e, c0=c0, ncp=ncp, wb=wb: e.dma_start(
                    out=wb[:, 0:ncp, :], in_=WD[c0 * 128:(c0 + ncp) * 128, :].rearrange("(c p) n -> p c n", p=128)),
                    writes=[f"wdb{pi % 2}"])
                for ci in range(ncp):
                    c = c0 + ci
                    b = wcount[0] % 2
                    wcount[0] += 1
                    S.dma("pool", lambda e, c=c, b=b: e.dma_start(
                        out=wgu[b][:, 0, :, :], in_=WG[:, c * 128:(c + 1) * 128].rearrange("(k p) f -> p k f", p=128)),
                        writes=[f"wgu{b}g"])
                    S.dma("pool", lambda e, c=c, b=b: e.dma_start(
                        out=wgu[b][:, 1, :, :], in_=WU[:, c * 128:(c + 1) * 128].rearrange("(k p) f -> p k f", p=128)),
                        writes=[f"wgu{b}u"])
                    for g, (t0, n) in enumerate(GROUPS):
                        pb = g % 2
                        pg, pu = PS[pb], PS[2 + pb]

                        def gm(e, b=b, t0=t0, n=n, pg=pg, j=0):
                            for k in range(KC):
                                r = e.matmul(pg[:, :n], lhsT=wgu[b][:, j, k, :], rhs=xnT[:, k, t0:t0 + n],
                                             start=(k == 0), stop=(k == KC - 1))
                            return r
                        S.op("pe", gm, reads=[f"wgu{b}g", f"xn{g}"], writes=[f"ps{pb}"])
                        S.op("pe", lambda e, b=b, t0=t0, n=n, pu=pu: gm(e, b, t0, n, pu, 1),
                             reads=[f"wgu{b}u", f"xn{g}"], writes=[f"ps{2 + pb}"])
                        S.op("act", lambda e, pb=pb, n=n, pg=pg: e.activation(out=sg[pb][:, :n], in_=pg[:, :n],
                                                                             func=AF.Silu),
                             reads=[f"ps{pb}"], writes=[f"sg{pb}"])
                        S.op("dve", lambda e, pb=pb, n=n, pu=pu, ci=ci, t0=t0: e.tensor_tensor(
                            out=hT[:, ci, t0:t0 + n], in0=sg[pb][:, :n], in1=pu[:, :n], op=ALU.mult),
                            reads=[f"sg{pb}", f"ps{2 + pb}"], writes=[f"hT{ci}_{g}"])
                for g, (t0, n) in enumerate(GROUPS):
                    for nn in range(KC):
                        pb = 4 + (nn % 2)
                        po = PS[pb]

                        def dm(e, nn=nn, t0=t0, n=n, po=po, wb=wb, ncp=ncp):
                            for ci in range(ncp):
                                r = e.matmul(po[:, :n], lhsT=wb[:, ci, nn * 128:(nn + 1) * 128],
                                             rhs=hT[:, ci, t0:t0 + n], start=(ci == 0), stop=(ci == ncp - 1))
                            return r
                        S.op("pe", dm, reads=[f"wdb{pi % 2}"] + [f"hT{ci}_{g}" for ci in range(ncp)],
                             writes=[f"ps{pb}"])
                        S.op("dve", lambda e, nn=nn, t0=t0, n=n, po=po: e.scalar_tensor_tensor(
                            out=xT[:, nn, t0:t0 + n], in0=po[:, :n], scalar=0.5, in1=xT[:, nn, t0:t0 + n],
                            op0=ALU.mult, op1=ALU.add),
                            reads=[f"ps{pb}", f"xT{g}"], writes=[f"xT{g}"])


        for l_ in range(DEPTH):
            for c_ in range(2):
                S.dma("sp", lambda e, l_=l_, c_=c_: e.dma_start(
                    out=scw[:, l_, c_, :], in_=sc_conv_w[l_][:, c_ * 128:(c_ + 1) * 128].rearrange("i p -> p i"),
                    allow_slow_non_contiguous=True), writes=["scw"])

        def proj_fm(l, col0, ncols, dst, dkey, eng="act"):
            b = wcount[0] % 2
            wcount[0] += 1
            S.dma("pool", lambda e: e.dma_start(
                out=wcol[b][:, :, 0:ncols], in_=w_in[l][:, col0:col0 + ncols].rearrange("(k p) f -> p k f", p=128)),
                writes=[f"wc{b}"])
            for g, (t0, n) in enumerate(GROUPS):
                pb = g % 2
                pp = PS[pb]

                def pm(e, t0=t0, n=n, pp=pp):
                    for k in range(KC):
                        r = e.matmul(pp[0:ncols, :n], lhsT=wcol[b][:, k, 0:ncols], rhs=xnT[:, k, t0:t0 + n],
                                     start=(k == 0), stop=(k == KC - 1))
                    return r
                S.op("pe", pm, reads=[f"wc{b}", f"xn{g}"], writes=[f"ps{pb}"])
                if (g % 2 == 0) == (eng == "act"):
                    S.op("act", lambda e, t0=t0, n=n, pp=pp: e.copy(out=dst[0:ncols, t0:t0 + n], in_=pp[0:ncols, :n]),
                         reads=[f"ps{pb}"], writes=[dkey])
                else:
                    S.op("dve", lambda e, t0=t0, n=n, pp=pp: e.tensor_copy(out=dst[0:ncols, t0:t0 + n],
                                                                          in_=pp[0:ncols, :n]),
                         reads=[f"ps{pb}"], writes=[dkey])

        def sconv_mixer(l):
            pA, pB, pC, pD = pbuf["A"], pbuf["B"], pbuf["C"], pbuf["D"]
            ups = tmpw[:, 0:96].rearrange("p (s t) -> p s t", s=NSQ)
            ys = tmpw[:, 96:160].rearrange("p (s t) -> p s t", s=NSQ)
            for c in range(2):
                proj_fm(l, 2056 + c * 128, 128, pA, "pA")
                proj_fm(l, 2312 + c * 128, 128, pB, "pB", eng="dve")
                proj_fm(l, 2568 + c * 128, 128, pC, "pC")
                S.op("pool", lambda e: e.memset(pD[:, 0:2], 0.0), writes=["pD"])
                S.op("dve", lambda e: e.tensor_tensor(out=pD[:, 2:2 + TP], in0=pA[:, 0:TP], in1=pB[:, 0:TP],
                                                      op=ALU.mult), reads=["pA", "pB", "pD"], writes=["pD"])
                for t_ in range(2):
                    S.dma("sp", lambda e, c=c, t_=t_: e.dma_start(
                        out=ups[:, :, t_], in_=state_sconv[l][:, t_, c * 128:(c + 1) * 128].rearrange("s p -> p s"),
                        allow_slow_non_contiguous=True), writes=["ups"])
                S.op("pool", lambda e: e.tensor_tensor(
                    out=ups[:, :, 2:6], in0=pA[:, TP:T].rearrange("p (t s) -> p s t", s=NSQ),
                    in1=pB[:, TP:T].rearrange("p (t s) -> p s t", s=NSQ), op=ALU.mult),
                    reads=["pA", "pB", "ups"], writes=["ups"])
                S.dma("sp", lambda e, c=c: e.dma_start(
                    out=p_sconv[l][:, c * 128:(c + 1) * 128].rearrange("t p -> p t"), in_=pD[:, TP:TP + 2],
                    allow_slow_non_contiguous=True), reads=["pD"])
                for t_ in range(2):
                    S.dma("sp", lambda e, c=c, t_=t_: e.dma_start(
                        out=s_sconv[l][:, t_, c * 128:(c + 1) * 128].rearrange("s p -> p s"), in_=ups[:, :, 4 + t_],
                        allow_slow_non_contiguous=True), reads=["ups"])
                S.op("dve", lambda e, c=c: e.tensor_scalar(out=pA[:, 0:TP], in0=pD[:, 0:TP], scalar1=scw[:, l, c, 0:1],
                                                           scalar2=None, op0=ALU.mult),
                     reads=["pD", "scw"], writes=["pA"])
                for i in (1, 2):
                    S.op("dve", lambda e, c=c, i=i: e.scalar_tensor_tensor(
                        out=pA[:, 0:TP], in0=pD[:, i:i + TP], scalar=scw[:, l, c, i:i + 1], in1=pA[:, 0:TP],
                        op0=ALU.mult, op1=ALU.add), reads=["pD", "scw", "pA"], writes=["pA"])
                S.op("pool", lambda e, c=c: e.tensor_tensor(out=mixT[:, 4 + c, 0:TP], in0=pA[:, 0:TP], in1=pC[:, 0:TP],
                                                            op=ALU.mult), reads=["pA", "pC"], writes=[f"mix{4 + c}"])
                S.op("dve", lambda e, c=c: e.tensor_scalar(out=ys, in0=ups[:, :, 0:4], scalar1=scw[:, l, c, 0:1],
                                                           scalar2=None, op0=ALU.mult),
                     reads=["ups", "scw"], writes=["ys"])
                for i in (1, 2):
                    S.op("dve", lambda e, c=c, i=i: e.scalar_tensor_tensor(
                        out=ys, in0=ups[:, :, i:i + 4], scalar=scw[:, l, c, i:i + 1], in1=ys,
                        op0=ALU.mult, op1=ALU.add), reads=["ups", "scw", "ys"], writes=["ys"])
                S.op("dve", lambda e, c=c: e.tensor_tensor(
                    out=mixT[:, 4 + c, TP:T].rearrange("p (t s) -> p s t", s=NSQ), in0=ys,
                    in1=pC[:, TP:T].rearrange("p (t s) -> p s t", s=NSQ), op=ALU.mult),
                    reads=["ys", "pC"], writes=[f"mix{4 + c}s"])


        def gdn_gates(l):
            b = wcount[0] % 2
            wcount[0] += 1
            S.dma("pool", lambda e: e.dma_start(out=wcol[b][:, :, 0:8],
                                                in_=w_in[l][:, 2048:2056].rearrange("(k p) f -> p k f", p=128)),
                  writes=[f"wc{b}"])

            def gate_math(src, n, dst, nl, tkey, dkey):
                tm = tmpw[0:n, 0:nl * 8].rearrange("p (s f) -> p s f", s=nl)
                al = albt[0:n, l, :].rearrange("p (o f) -> p o f", o=1).to_broadcast([n, nl, 4])
                dt = dtbt[0:n, l, :].rearrange("p (o f) -> p o f", o=1).to_broadcast([n, nl, 4])
                S.op("act", lambda e: e.activation(out=dst[:, :, 0:4], in_=src[:, :, 0:4], func=AF.Sigmoid),
                     reads=[tkey], writes=[dkey])
                S.op("dve", lambda e: e.tensor_tensor(out=tm[:, :, 0:4], in0=src[:, :, 4:8], in1=dt, op=ALU.add),
                     reads=[tkey, "dtbt"], writes=["tmpw"])
                S.op("dve", lambda e: e.tensor_scalar(out=tm[:, :, 4:8], in0=tm[:, :, 0:4], scalar1=-1.0, scalar2=None,
                                                      op0=ALU.mult), reads=["tmpw"], writes=["tmpw"])
                S.op("dve", lambda e: e.tensor_tensor(out=tm[:, :, 4:8], in0=tm[:, :, 0:4], in1=tm[:, :, 4:8], op=ALU.max),
                     reads=["tmpw"], writes=["tmpw"])
                S.op("act", lambda e: e.activation(out=tm[:, :, 4:8], in_=tm[:, :, 4:8], func=AF.Exp, scale=-1.0),
                     reads=["tmpw"], writes=["tmpw"])
                S.op("act", lambda e: e.activation(out=tm[:, :, 4:8], in_=tm[:, :, 4:8], func=AF.Ln, bias=onecol[0:n, 0:1],
                                                   scale=1.0), reads=["tmpw", "onecol"], writes=["tmpw"])
                S.op("dve", lambda e: e.scalar_tensor_tensor(out=tm[:, :, 0:4], in0=tm[:, :, 0:4], scalar=0.0,
                                                             in1=tm[:, :, 4:8], op0=ALU.max, op1=ALU.add),
                     reads=["tmpw"], writes=["tmpw"])
                S.op("dve", lambda e: e.tensor_tensor(out=dst[:, :, 4:8], in0=tm[:, :, 0:4], in1=al, op=ALU.mult),
                     reads=["tmpw", "albt"], writes=[dkey])
            for tt in range(16):
                def bm(e, tt=tt):
                    for k in range(KC):
                        r = e.matmul(PS[0][:, 0:8], lhsT=xnT[:, k, tt * 128:(tt + 1) * 128], rhs=wcol[b][:, k, 0:8],
                                     start=(k == 0), stop=(k == KC - 1))
                    return r
                S.op("pe", bm, reads=[f"wc{b}", f"xn{tt // 4}"], writes=["ps0"])
                gate_math(PS[0][:, 0:8].rearrange("p (s f) -> p s f", s=1), 128,
                          bg[:, tt, :].rearrange("p (s f) -> p s f", s=1), 1, "ps0", "bg")

            def bms(e):
                xs = [xnT[:, k, TP:T].rearrange("p (t s) -> p s t", s=NSQ) for k in range(KC)]
                for s_ in range(NSQ):
                    for k in range(KC):
                        r = e.matmul(PS[1][0:4, s_ * 8:(s_ + 1) * 8], lhsT=xs[k][:, s_, :], rhs=wcol[b][:, k, 0:8],
                                     start=(k == 0), stop=(k == KC - 1))
                return r
            S.op("pe", bms, reads=[f"wc{b}", "xn4"], writes=["ps1"])
            gate_math(PS[1][0:4, 0:NSQ * 8].rearrange("p (s f) -> p s f", s=NSQ), 4, bgs[:, :, :], NSQ, "ps1", "bgs")

        def l2norm(buf, bkey, scale):
            for g, (t0, n) in enumerate(GROUPS):
                S.op("act", lambda e, t0=t0, n=n: e.activation(out=acc[:, :n], in_=buf[:, t0:t0 + n], func=AF.Square),
                     reads=[bkey], writes=["acc"])
                S.op("pe", lambda e, n=n: e.matmul(PS[5][:, :n], lhsT=ones[:], rhs=acc[:, :n], start=True, stop=True),
                     reads=["ones", "acc"], writes=["ps5"])
                S.op("act", lambda e, n=n: e.activation(out=rstd[:, :n], in_=PS[5][:, :n], func=AF.Sqrt,
                                                       bias=epsb[:, 0:1], scale=1.0), reads=["ps5", "epsb"],
                     writes=["rstd"])
                S.op("dve", lambda e, n=n: e.reciprocal(out=rstd[:, :n], in_=rstd[:, :n]), reads=["rstd"],
                     writes=["rstd"])
                S.op("dve", lambda e, t0=t0, n=n: e.scalar_tensor_tensor(
                    out=buf[:, t0:t0 + n], in0=buf[:, t0:t0 + n], scalar=scale, in1=rstd[:, :n], op0=ALU.mult,
                    op1=ALU.mult), reads=[bkey, "rstd"], writes=[bkey])

        def gdn_conv(l, h, j, buf, bkey):
            c = j * 4 + h
            w = lambda i: gcw[:, l, c, i:i + 1]
            csj = cs[:, j, :, :]
            for t_ in range(3):
                S.dma("sp", lambda e, t_=t_: e.dma_start(
                    out=csj[:, :, t_], in_=state_gdn_conv[l][:, t_, c * 128:(c + 1) * 128].rearrange("s p -> p s"),
                    allow_slow_non_contiguous=True), writes=[f"cs{j}"])
            xs = buf[:, TP:T].rearrange("p (t s) -> p s t", s=NSQ)
            S.op("pool", lambda e: e.tensor_copy(out=csj[:, :, 3:7], in_=xs), reads=[bkey, f"cs{j}"], writes=[f"cs{j}"])
            ys = tmpw[:, 512 + j * 64:512 + (j + 1) * 64].rearrange("p (s t) -> p s t", s=NSQ)
            S.op("dve", lambda e: e.tensor_scalar(out=ys, in0=csj[:, :, 3:7], scalar1=w(3), scalar2=None, op0=ALU.mult),
                 reads=[f"cs{j}", "gcw"], writes=[f"ys{j}"])
            for i in range(3):
                S.op("dve", lambda e, i=i: e.scalar_tensor_tensor(out=ys, in0=csj[:, :, i:i + 4], scalar=w(i), in1=ys,
                                                                  op0=ALU.mult, op1=ALU.add),
                     reads=[f"cs{j}", "gcw", f"ys{j}"], writes=[f"ys{j}"])
            S.op("act", lambda e: e.activation(out=xs, in_=ys, func=AF.Silu), reads=[f"ys{j}"], writes=[bkey])
            for blk in (3, 2, 1, 0):
                t0 = blk * 512
                tb = sg[blk % 2]
                S.op("dve", lambda e, t0=t0, tb=tb: e.tensor_scalar(out=tb[:, :], in0=buf[:, t0:t0 + 512], scalar1=w(3),
                                                                    scalar2=None, op0=ALU.mult),
                     reads=[bkey, "gcw"], writes=[f"sg{blk % 2}"])
                for i in range(3):
                    sh = 3 - i
                    lo = sh if blk == 0 else 0
                    S.op("dve", lambda e, t0=t0, tb=tb, i=i, sh=sh, lo=lo: e.scalar_tensor_tensor(
                        out=tb[:, lo:512], in0=buf[:, t0 + lo - sh:t0 + 512 - sh], scalar=w(i), in1=tb[:, lo:512],
                        op0=ALU.mult, op1=ALU.add), reads=[bkey, "gcw", f"sg{blk % 2}"], writes=[f"sg{blk % 2}"])
                S.op("act", lambda e, t0=t0, tb=tb: e.activation(out=buf[:, t0:t0 + 512], in_=tb[:, :], func=AF.Silu),
                     reads=[f"sg{blk % 2}"], writes=[bkey])

        TS_ = [sg[0], sg[1], sq[0], sq[1], acc, rstd, xtok[0][:, 0:512], xtok[0][:, 512:1024],
               xtok[1][:, 0:512], xtok[1][:, 512:1024], None, None, av(WB, 512), av(WB + 512, 512)]

        def gdn_chunk(C, NL, qT, kT, vT, Sap, beta, g, oT, skey, okey):
            W = NL * C
            Tt = list(TS_)
            Tt[10] = av(4224, 512)
            Tt[11] = av(4736, 512)
            tk = ["sg0", "sg1", "sq0", "sq1", "acc", "rstd", "xtok0a", "xtok0b", "xtok1a", "xtok1b", "scrA", "scrB", "wc0", "wc1"]
            SM = Tt[0]
            gc, egc, nbeta, wt = (SM[0:C, i * NL:(i + 1) * NL] for i in range(4))
            v3 = lambda t, p=C: t[0:p, 0:W].rearrange("p (l c) -> p l c", l=NL)
            vk = lambda t: t[0:C, 0:NL * 128].rearrange("p (l d) -> p l d", l=NL)
            bc_l = lambda col: col.rearrange("p (l o) -> p l o", o=1).to_broadcast([C, NL, C])
            bc_m = lambda m: m[0:C, 0:C].rearrange("p (o c) -> p o c", o=1).to_broadcast([C, NL, C])
            rd = ["pA", "pB", "pC"]
            S.op("pe", lambda e: e.matmul(PS[0][0:C, 0:NL], lhsT=triuf[0:C, 0:C], rhs=g, start=True, stop=True),
                 reads=["triuf", "bg", "bgs"], writes=["ps0"])
            S.op("dve", lambda e: e.tensor_copy(out=gc, in_=PS[0][0:C, 0:NL]), reads=["ps0"], writes=[tk[0]])
            S.op("act", lambda e: e.activation(out=egc, in_=PS[0][0:C, 0:NL], func=AF.Exp), reads=["ps0"], writes=[tk[0]])
            S.op("dve", lambda e: e.tensor_scalar(out=nbeta, in0=beta, scalar1=-1.0, scalar2=None, op0=ALU.mult),
                 reads=["bg", "bgs"], writes=[tk[0]])
            S.op("dve", lambda e: e.tensor_tensor(out=v3(Tt[1]), in0=bc_m(ident), in1=bc_l(gc), op=ALU.mult),
                 reads=["ident", tk[0]], writes=[tk[1]])
            S.op("pe", lambda e: e.matmul(PS[1][:, 0:W], lhsT=ones[0:C, :], rhs=Tt[1][0:C, 0:W], start=True, stop=True),
                 reads=["ones", tk[1]], writes=["ps1"])
            S.op("dve", lambda e: e.tensor_copy(out=Tt[2][:, 0:W], in_=PS[1][:, 0:W]), reads=["ps1"], writes=[tk[2]])
            S.op("act", lambda e: e.activation(out=Tt[8][:, 0:W], in_=PS[1][:, 0:W], func=AF.Exp), reads=["ps1"],
                 writes=[tk[8]])
            S.op("dve", lambda e: e.tensor_tensor(out=v3(Tt[1]), in0=bc_m(ident), in1=bc_l(beta), op=ALU.mult),
                 reads=["ident", "bg", "bgs", tk[1]], writes=[tk[1]])
            S.op("pe", lambda e: e.matmul(PS[2][0:C, 0:W], lhsT=ones[0:C, 0:C], rhs=Tt[1][0:C, 0:W], start=True, stop=True),
                 reads=["ones", tk[1]], writes=["ps2"])
            S.op("act", lambda e: e.copy(out=Tt[3][0:C, 0:W], in_=PS[2][0:C, 0:W]), reads=["ps2"], writes=[tk[3]])
            S.op("dve", lambda e: e.tensor_tensor(out=wt, in0=v3(Tt[2])[:, :, C - 1], in1=gc, op=ALU.subtract),
                 reads=[tk[2], tk[0]], writes=[tk[0]])
            S.op("act", lambda e: e.activation(out=wt, in_=wt, func=AF.Exp), reads=[tk[0]], writes=[tk[0]])

            def kkm(e):
                for ln in range(NL):
                    e.matmul(PS[3][0:C, ln * C:(ln + 1) * C], lhsT=kT(ln), rhs=kT(ln), start=True, stop=True)
                    r = e.matmul(PS[4][0:C, ln * C:(ln + 1) * C], lhsT=kT(ln), rhs=qT(ln), start=True, stop=True)
                return r
            S.op("pe", kkm, reads=rd, writes=["ps3", "ps4"])
            S.op("dve", lambda e: e.tensor_tensor(out=v3(Tt[4]), in0=bc_l(gc), in1=v3(Tt[2]), op=ALU.subtract),
                 reads=[tk[0], tk[2]], writes=[tk[4]])
            S.op("pool", lambda e: e.tensor_tensor(out=v3(Tt[4]), in0=v3(Tt[4]), in1=bc_m(mNL), op=ALU.add),
                 reads=[tk[4], "mNL"], writes=[tk[4]])
            S.op("act", lambda e: e.activation(out=v3(Tt[4]), in_=v3(Tt[4]), func=AF.Exp), reads=[tk[4]], writes=[tk[4]])
            S.op("pool", lambda e: e.tensor_tensor(out=v3(Tt[4]), in0=v3(Tt[4]), in1=bc_m(sL), op=ALU.mult),
                 reads=[tk[4], "sL"], writes=[tk[4]])
            S.op("dve", lambda e: e.tensor_tensor(out=v3(Tt[1]), in0=PS[3][0:C, 0:W].rearrange("p (l c) -> p l c", l=NL),
                                                  in1=bc_l(beta), op=ALU.mult),
                 reads=["ps3", "bg", "bgs", tk[1]], writes=[tk[1]])
            S.op("pool", lambda e: e.tensor_tensor(out=v3(Tt[4]), in0=v3(Tt[4]), in1=v3(Tt[1]), op=ALU.mult),
                 reads=[tk[4], tk[1]], writes=[tk[4]])
            S.op("dve", lambda e: e.tensor_tensor(out=v3(Tt[5]), in0=v3(Tt[2]), in1=bc_l(gc), op=ALU.subtract),
                 reads=[tk[0], tk[2]], writes=[tk[5]])
            S.op("pool", lambda e: e.tensor_tensor(out=v3(Tt[5]), in0=v3(Tt[5]), in1=bc_m(mNU), op=ALU.add),
                 reads=[tk[5], "mNU"], writes=[tk[5]])
            S.op("act", lambda e: e.activation(out=v3(Tt[5]), in_=v3(Tt[5]), func=AF.Exp), reads=[tk[5]], writes=[tk[5]])
            S.op("dve", lambda e: e.tensor_tensor(out=v3(Tt[6]), in0=PS[4][0:C, 0:W].rearrange("p (l c) -> p l c", l=NL),
                                                  in1=v3(Tt[5]), op=ALU.mult), reads=["ps4", tk[5]], writes=[tk[6]])
            S.op("pool", lambda e: e.tensor_tensor(out=v3(Tt[5]), in0=v3(Tt[5]), in1=bc_m(sU), op=ALU.mult),
                 reads=[tk[5], "sU"], writes=[tk[5]])
            S.op("dve", lambda e: e.tensor_tensor(out=v3(Tt[1]), in0=PS[3][0:C, 0:W].rearrange("p (l c) -> p l c", l=NL),
                                                  in1=v3(Tt[3]), op=ALU.mult), reads=["ps3", tk[3], tk[1]], writes=[tk[1]])
            S.op("pool", lambda e: e.tensor_tensor(out=v3(Tt[5]), in0=v3(Tt[5]), in1=v3(Tt[1]), op=ALU.mult),
                 reads=[tk[5], tk[1]], writes=[tk[5]])
            S.op("dve", lambda e: e.tensor_tensor(out=v3(Tt[7]), in0=bc_m(ident), in1=v3(Tt[5]), op=ALU.subtract),
                 reads=["ident", tk[5]], writes=[tk[7]])
            nlev = max(1, int(np.ceil(np.log2(C))) - 1)
            P, PT, Pn, PTn = 4, 5, 1, 3
            for lev in range(nlev):
                last = lev == nlev - 1

                def sqm(e, P=P, PT=PT, last=last):
                    for ln in range(NL):
                        sl = slice(ln * C, (ln + 1) * C)
                        r = e.matmul(PS[5][0:C, sl], lhsT=Tt[PT][0:C, sl], rhs=Tt[P][0:C, sl], start=True, stop=True)
                        if not last:
                            r = e.matmul(PS[6][0:C, sl], lhsT=Tt[P][0:C, sl], rhs=Tt[PT][0:C, sl], start=True, stop=True)
                    return r
                S.op("pe", sqm, reads=[tk[P], tk[PT]], writes=["ps5"] + ([] if last else ["ps6"]))
                S.op("act", lambda e, Pn=Pn: e.copy(out=Tt[Pn][0:C, 0:W], in_=PS[5][0:C, 0:W]), reads=["ps5"],
                     writes=[tk[Pn]])
                if not last:
                    S.op("dve", lambda e, PTn=PTn: e.tensor_copy(out=Tt[PTn][0:C, 0:W], in_=PS[6][0:C, 0:W]),
                         reads=["ps6"], writes=[tk[PTn]])

                def ym(e, Pn=Pn):
                    for ln in range(NL):
                        sl = slice(ln * C, (ln + 1) * C)
                        r = e.matmul(PS[7][0:C, sl], lhsT=Tt[Pn][0:C, sl], rhs=Tt[7][0:C, sl], start=True, stop=True)
                    return r
                S.op("pe", ym, reads=[tk[Pn], tk[7]], writes=["ps7"])
                S.op("dve", lambda e: e.tensor_tensor(out=Tt[7][0:C, 0:W], in0=Tt[7][0:C, 0:W], in1=PS[7][0:C, 0:W],
                                                      op=ALU.add), reads=[tk[7], "ps7"], writes=[tk[7]])
                P, PT, Pn, PTn = Pn, PTn, P, PT
            for ln in range(NL):
                S.op("dve", lambda e, ln=ln: e.tensor_tensor(out=Tt[9][:, ln * C:(ln + 1) * C], in0=qT(ln),
                                                             in1=Tt[8][:, ln * C:(ln + 1) * C], op=ALU.mult),
                     reads=["pA", tk[8]], writes=[tk[9]])

            def trk(e):
                for ln in range(NL):
                    e.transpose(out=PS[0][0:C, ln * 128:(ln + 1) * 128], in_=kT(ln), identity=ident[:])
                    r = e.transpose(out=PS[1][0:C, ln * 128:(ln + 1) * 128], in_=vT(ln), identity=ident[:])
                return r
            S.op("pe", trk, reads=rd + ["ident"], writes=["ps0", "ps1"])
            S.op("dve", lambda e: e.tensor_tensor(
                out=vk(Tt[10]), in0=PS[0][0:C, 0:NL * 128].rearrange("p (l d) -> p l d", l=NL),
                in1=wt.rearrange("p (l o) -> p l o", o=1).to_broadcast([C, NL, 128]), op=ALU.mult),
                reads=["ps0", tk[0]], writes=[tk[10]])
            S.op("act", lambda e: e.copy(out=Tt[11][0:C, 0:NL * 128], in_=PS[1][0:C, 0:NL * 128]), reads=["ps1"],
                 writes=[tk[11]])
            for ln in range(NL):
                sl = slice(ln * C, (ln + 1) * C)
                dl = slice(ln * 128, (ln + 1) * 128)
                Sl = Sap(ln)
                S.op("pe", lambda e, ln=ln, Sl=Sl: e.matmul(PS[2][0:C, 0:128], lhsT=kT(ln), rhs=Sl, start=True, stop=True),
                     reads=["pB", skey], writes=["ps2"])
                S.op("dve", lambda e, ln=ln, dl=dl: e.scalar_tensor_tensor(
                    out=Tt[12][0:C, dl], in0=PS[2][0:C, 0:128], scalar=egc[:, ln:ln + 1], in1=Tt[11][0:C, dl],
                    op0=ALU.mult, op1=ALU.subtract), reads=["ps2", tk[0], tk[11]], writes=[tk[12]])
                S.op("dve", lambda e, ln=ln, dl=dl: e.tensor_scalar(out=Tt[12][0:C, dl], in0=Tt[12][0:C, dl],
                                                                    scalar1=nbeta[:, ln:ln + 1], scalar2=None,
                                                                    op0=ALU.mult), reads=[tk[12], tk[0]], writes=[tk[12]])
                S.op("pe", lambda e, sl=sl, dl=dl: e.matmul(PS[3][0:C, 0:128], lhsT=Tt[7][0:C, sl], rhs=Tt[12][0:C, dl],
                                                          start=True, stop=True), reads=[tk[7], tk[12]], writes=["ps3"])
                S.op("act", lambda e, dl=dl: e.copy(out=Tt[13][0:C, dl], in_=PS[3][0:C, 0:128]), reads=["ps3"],
                     writes=[tk[13]])

                def om(e, sl=sl, dl=dl, Sl=Sl):
                    e.matmul(PS[4][:, 0:C], lhsT=Sl, rhs=Tt[9][:, sl], start=True, stop=False)
                    return e.matmul(PS[4][:, 0:C], lhsT=Tt[13][0:C, dl], rhs=Tt[6][0:C, sl], start=False, stop=True)
                S.op("pe", om, reads=[skey, tk[9], tk[13], tk[6]], writes=["ps4"])
                S.op("act", lambda e, ln=ln: e.copy(out=oT(ln), in_=PS[4][:, 0:C]), reads=["ps4"], writes=[okey])
                S.op("pe", lambda e, dl=dl: e.matmul(PS[5][:, 0:128], lhsT=Tt[10][0:C, dl], rhs=Tt[13][0:C, dl],
                                                    start=True, stop=True), reads=[tk[10], tk[13]], writes=["ps5"])
                S.op("dve", lambda e, ln=ln, Sl=Sl: e.scalar_tensor_tensor(
                    out=Sl, in0=Sl, scalar=Tt[8][:, ln * C + C - 1:ln * C + C], in1=PS[5][:, 0:128], op0=ALU.mult,
                    op1=ALU.add), reads=[skey, tk[8], "ps5"], writes=[skey])

        def gdn_mixer(l):
            pA, pB, pC, pD = pbuf["A"], pbuf["B"], pbuf["C"], pbuf["D"]
            gdn_gates(l)
            for h in range(4):
                proj_fm(l, h * 128, 128, pA, "pA")
                proj_fm(l, 512 + h * 128, 128, pB, "pB", eng="dve")
                proj_fm(l, 1024 + h * 128, 128, pC, "pC")
                proj_fm(l, 1536 + h * 128, 128, pD, "pD", eng="dve")
                for j, (buf, key) in enumerate(((pA, "pA"), (pB, "pB"), (pC, "pC"))):
                    gdn_conv(l, h, j, buf, key)
                l2norm(pA, "pA", 128 ** -0.5)
                l2norm(pB, "pB", 1.0)
                S.op("act", lambda e: e.activation(out=pD[:, :], in_=pD[:, :], func=AF.Silu), reads=["pD"], writes=["pD"])
                S.op("pool", lambda e: e.memset(Sp[:], 0.0), writes=["Sp"])
                for n in range(16):
                    cl = slice(n * 128, (n + 1) * 128)
                    gdn_chunk(128, 1, lambda ln, cl=cl: pA[:, cl], lambda ln, cl=cl: pB[:, cl], lambda ln, cl=cl: pC[:, cl],
                              lambda ln: Sp[:, :], bg[:, n, h:h + 1], bg[:, n, 4 + h:5 + h],
                              lambda ln, cl=cl: pA[:, cl], "Sp", "pA")
                S.dma("sp", lambda e, h=h: e.dma_start(out=p_gdn[l][h], in_=Sp[:, :]), reads=["Sp"])
                smp = lambda buf: buf[:, TP:T].rearrange("p (t s) -> p s t", s=NSQ)
                for sbi in range(4):
                    s0 = sbi * 4
                    S.dma("act", lambda e, s0=s0, h=h: e.dma_start(
                        out=Ssm[:, :, :], in_=state_gdn[l][s0:s0 + 4, h].rearrange("s k v -> k s v")), writes=["Ssm"])
                    gdn_chunk(4, 4, lambda ln, s0=s0: smp(pA)[:, s0 + ln, :], lambda ln, s0=s0: smp(pB)[:, s0 + ln, :],
                              lambda ln, s0=s0: smp(pC)[:, s0 + ln, :], lambda ln: Ssm[:, ln, :],
                              bgs[:, s0:s0 + 4, h], bgs[:, s0:s0 + 4, 4 + h],
                              lambda ln, s0=s0: smp(pA)[:, s0 + ln, :], "Ssm", "pA")
                    S.dma("act", lambda e, s0=s0, h=h: e.dma_start(
                        out=s_gdn[l][s0:s0 + 4, h].rearrange("s k v -> k s v"), in_=Ssm[:, :, :]), reads=["Ssm"])
                for g, (t0, n) in enumerate(GROUPS):
                    S.op("act", lambda e, t0=t0, n=n: e.activation(out=acc[:, :n], in_=pA[:, t0:t0 + n], func=AF.Square),
                         reads=["pA"], writes=["acc"])
                    S.op("pe", lambda e, n=n: e.matmul(PS[5][:, :n], lhsT=ones[:], rhs=acc[:, :n], start=True, stop=True),
                         reads=["ones", "acc"], writes=["ps5"])
                    S.op("act", lambda e, n=n: e.activation(out=rstd[:, :n], in_=PS[5][:, :n], func=AF.Sqrt,
                                                           bias=epsb[:, 0:1], scale=1.0 / 128), reads=["ps5", "epsb"],
                         writes=["rstd"])
                    S.op("dve", lambda e, n=n: e.reciprocal(out=rstd[:, :n], in_=rstd[:, :n]), reads=["rstd"], writes=["rstd"])
                    S.op("dve", lambda e, t0=t0, n=n: e.scalar_tensor_tensor(
                        out=acc[:, :n], in0=pA[:, t0:t0 + n], scalar=gnw[:, l:l + 1], in1=rstd[:, :n], op0=ALU.mult,
                        op1=ALU.mult), reads=["pA", "rstd", "gnw", "acc"], writes=["acc"])
                    S.op("dve", lambda e, t0=t0, n=n, h=h: e.tensor_tensor(
                        out=mixT[:, h, t0:t0 + n], in0=acc[:, :n], in1=pD[:, t0:t0 + n], op=ALU.mult),
                        reads=["acc", "pD"], writes=[f"mix{h}", f"mix{h}s"])

        qsS = tmpw[:, 512:640].rearrange("p (h c) -> p h c", h=2)
        ksS = tmpw[:, 640:768].rearrange("p (h c) -> p h c", h=2)

        def moba_prompt(l):
            pA, pB = pbuf["A"], pbuf["B"]
            qb = av(WB + 1024 + 2 * 2112, 1056, BF16)
            kb = av(WB + 1024 + 2 * 2112 + 1056, 1056, BF16)
            Pb = [sq[i][:, 0:256].bitcast(BF16).rearrange("p (j q) -> p j q", j=4) for i in range(2)]
            for hp in range(2):
                proj_fm(l, 2824 + hp * 128, 128, pA, "pA")
                proj_fm(l, 3080 + hp * 128, 128, pB, "pB", eng="dve")
                S.op("act", lambda e: e.mul(out=qb[:, :], in_=pA[:, :], mul=0.125),
                     reads=["pA"], writes=["qb"])
                S.op("pool", lambda e, hp=hp: e.tensor_copy(out=qsS[:, hp, :], in_=pA[:, TP:T]), reads=["pA"], writes=["qsS"])
                S.op("pool", lambda e, hp=hp: e.tensor_copy(out=ksS[:, hp, :], in_=pB[:, TP:T]), reads=["pB"], writes=["ksS"])
                S.op("pool", lambda e: e.tensor_copy(out=kb[:, :], in_=pB[:, :]), reads=["pB"], writes=["kb"])
                S.op("dve", lambda e: e.tensor_reduce(out=kmT[:, :], in_=pB[:, 0:TP].rearrange("p (n j) -> p n j", n=8),
                                                      axis=AX.X, op=ALU.add), reads=["pB"], writes=["kmT"])
                for qt in range(16):
                    bt = qt // 2
                    selT = tmpw[0:8, (qt % 2) * 256:(qt % 2) * 256 + 256].rearrange("p (h q) -> p h q", h=2)
                    for hh in range(2):
                        pb = 64 * hh
                        gt = acc[:, hh * 64:hh * 64 + 64]
                        S.op("pe", lambda e, pb=pb, qt=qt, hh=hh: e.matmul(
                            PS[6][:, hh * 8:hh * 8 + 8], lhsT=pA[pb:pb + 64, qt * 128:(qt + 1) * 128],
                            rhs=kmT[pb:pb + 64, :], start=True, stop=True),
                            reads=["pA", "kmT"], writes=["ps6"])
                        S.op("dve", lambda e, gt=gt, bt=bt, hh=hh: e.tensor_tensor(
                            out=gt[:, 0:8], in0=PS[6][:, hh * 8:hh * 8 + 8], in1=negm[:, bt, :], op=ALU.add),
                            reads=["ps6", "negm"], writes=[f"gt{hh}"])
                        S.op("dve", lambda e, gt=gt: e.max(out=gt[:, 8:16], in_=gt[:, 0:8]),
                             reads=[f"gt{hh}"], writes=[f"gt{hh}"])
                        S.op("dve", lambda e, gt=gt: e.tensor_scalar(out=gt[:, 16:24], in0=gt[:, 0:8], scalar1=gt[:, 10:11],
                                                                     scalar2=None, op0=ALU.is_ge),
                             reads=[f"gt{hh}"], writes=[f"gt{hh}"])
                        S.op("dve", lambda e, gt=gt, bt=bt: e.tensor_tensor(out=gt[:, 24:32], in0=gt[:, 16:24],
                                                                            in1=pastm[:, bt, :], op=ALU.mult),
                             reads=[f"gt{hh}", "pastm"], writes=[f"gt{hh}"])
                        S.op("dve", lambda e, gt=gt, bt=bt: e.tensor_tensor(out=gt[:, 32:40], in0=gt[:, 24:32],
                                                                            in1=ownm[:, bt, :], op=ALU.add),
                             reads=[f"gt{hh}", "ownm"], writes=[f"gt{hh}"])
                        S.op("dve", lambda e, gt=gt: e.tensor_scalar(out=gt[:, 40:48], in0=gt[:, 32:40], scalar1=-1.0,
                                                                     scalar2=1e30, op0=ALU.add, op1=ALU.mult),
                             reads=[f"gt{hh}"], writes=[f"gt{hh}"])
                        S.op("pe", lambda e, gt=gt, hh=hh: e.transpose(out=PS[7][0:8, hh * 128:(hh + 1) * 128],
                                                                       in_=gt[:, 40:48], identity=ident[:]),
                             reads=[f"gt{hh}", "ident"], writes=["ps7"])
                        S.op("act", lambda e, selT=selT, hh=hh: e.copy(out=selT[:, hh, :],
                                                                       in_=PS[7][0:8, hh * 128:(hh + 1) * 128]),
                             reads=["ps7"], writes=[f"selT{qt % 2}_{hh}"])
                    for hh in range(2):
                        pb = 64 * hh
                        h = 2 * hp + hh
                        nkt = qt + 1
                        po, psm = PS[4], PS[5]
                        for c0 in range(0, nkt, 4):
                            nk = min(4, nkt - c0)
                            bi = (c0 // 4) % 2
                            pss = PS[bi]

                            def sm(e, c0=c0, nk=nk, pss=pss, pb=pb, qt=qt, selT=selT, hh=hh):
                                for j in range(nk):
                                    kt = c0 + j
                                    e.matmul(pss[:, j * 128:(j + 1) * 128], lhsT=kb[pb:pb + 64, kt * 128:(kt + 1) * 128],
                                             rhs=qb[pb:pb + 64, qt * 128:(qt + 1) * 128], start=True, stop=False)
                                    r = e.matmul(pss[:, j * 128:(j + 1) * 128], lhsT=E8[0:8, kt // 2, :],
                                                 rhs=selT[:, hh, :], start=False, stop=True)
                                return r
                            S.op("pe", sm, reads=["kb", "qb", "E8", f"selT{qt % 2}_{hh}"], writes=[f"ps{bi}"])
                            S.op("act", lambda e, nk=nk, pss=pss, bi=bi: e.activation(
                                out=Pb[bi][:, 0:nk, :], in_=pss[:, 0:nk * 128].rearrange("p (j q) -> p j q", j=nk),
                                func=AF.Exp), reads=[f"ps{bi}"], writes=[f"Pb{bi}"])
                            if c0 + nk == nkt:
                                S.op("pool", lambda e, nk=nk, bi=bi: e.tensor_tensor(
                                    out=Pb[bi][:, nk - 1, :], in0=Pb[bi][:, nk - 1, :], in1=triuf[:, :], op=ALU.mult),
                                    reads=[f"Pb{bi}", "triuf"], writes=[f"Pb{bi}"])

                            def pv(e, c0=c0, nk=nk, bi=bi, pb=pb, h=h, nkt=nkt):
                                for j in range(nk):
                                    kt = c0 + j
                                    e.matmul(po[pb:pb + 64, 0:128], lhsT=Vb[:, kt, h * 64:(h + 1) * 64], rhs=Pb[bi][:, j, :],
                                             start=(kt == 0), stop=(kt == nkt - 1))
                                    r = e.matmul(psm[pb:pb + 64, 0:128], lhsT=onesb[:, 0:64], rhs=Pb[bi][:, j, :],
                                                 start=(kt == 0), stop=(kt == nkt - 1))
                                return r
                            S.op("pe", pv, reads=["pD", f"Pb{bi}", "onesb"], writes=["ps4", "ps5"])
                        rs = rstd[pb:pb + 64, 0:128]
                        S.op("dve", lambda e, rs=rs, pb=pb: e.reciprocal(out=rs, in_=psm[pb:pb + 64, 0:128]),
                             reads=["ps5"], writes=[f"rs{hh}"])
                        S.op("dve", lambda e, rs=rs, pb=pb, qt=qt, hp=hp: e.tensor_tensor(
                            out=mixT[pb:pb + 64, 6 + hp, qt * 128:(qt + 1) * 128], in0=po[pb:pb + 64, 0:128], in1=rs,
                            op=ALU.mult), reads=["ps4", f"rs{hh}"], writes=[f"mix{6 + hp}"])


        def moba_sample(l):
            Kpg = av(WB + 1024, 4096, F32, "p (j f) -> p j f", j=16)
            Vpg = av(WB + 1024 + 2 * 2112, 4096, F32, "p (j f) -> p j f", j=16)
            KT = av(0, 2048, BF16, "p (h k) -> p h k", h=2)
            Vb16 = av(2048, 2080, BF16, "p (j h d) -> p j h d", j=16, h=4)
            wv = av(4224, 1024, BF16, "p (k f) -> p k f", k=KC)
            ck = cache_k.rearrange("l n p h d -> (l n p) (h d)")
            cv = cache_v.rearrange("l n p h d -> (l n p) (h d)")
            pix = pidx if l == 0 else pidx1
            S.dma("pool", lambda e: e.dma_start(out=wv, in_=w_in[l][:, 3336:3592].rearrange("(k p) f -> p k f", p=128)),
                  writes=["wv"])
            S.op("pool", lambda e: e.memset(Vb16[:, :, :, 64:65], 1.0), writes=["Vb16"])
            sm = acc
            smb = rstd.bitcast(BF16) if False else None
            for s_ in range(NSQ):
                for j in range(16):
                    S.dma("pool", lambda e, s_=s_, j=j: e.indirect_dma_start(
                        out=Kpg[:, j, :], out_offset=None, in_=ck,
                        in_offset=bass.IndirectOffsetOnAxis(ap=pix[:, s_ * 16 + j:s_ * 16 + j + 1], axis=0)),
                        reads=["pidx", "pidx1"], writes=["pA", "pB"])
                for j in range(16):
                    S.dma("pool", lambda e, s_=s_, j=j: e.indirect_dma_start(
                        out=Vpg[:, j, :], out_offset=None, in_=cv,
                        in_offset=bass.IndirectOffsetOnAxis(ap=pix[:, s_ * 16 + j:s_ * 16 + j + 1], axis=0)),
                        reads=["pidx", "pidx1"], writes=["pC", "pD"])
                for hp in range(2):
                    for q4 in range(4):
                        pb = (hp * 4 + q4) % 2

                        def trm(e, hp=hp, q4=q4, pb=pb):
                            for jj in range(4):
                                j = q4 * 4 + jj
                                r = e.transpose(out=PS[pb][:, jj * 128:(jj + 1) * 128], in_=Kpg[:, j, hp * 128:(hp + 1) * 128],
                                                identity=ident[:])
                            return r
                        S.op("pe", trm, reads=["pA", "pB", "ident"], writes=[f"ps{pb}"])
                        if pb == 0:
                            S.op("act", lambda e, hp=hp, q4=q4: e.copy(out=KT[:, hp, q4 * 512:(q4 + 1) * 512], in_=PS[0][:, :]),
                                 reads=["ps0"], writes=["KT"])
                        else:
                            S.op("dve", lambda e, hp=hp, q4=q4: e.tensor_copy(out=KT[:, hp, q4 * 512:(q4 + 1) * 512],
                                                                             in_=PS[1][:, :]), reads=["ps1"], writes=["KT"])

                def kms(e):
                    for hp in range(2):
                        for j in range(16):
                            r = e.matmul(PS[2][:, hp * 8 + j // 2:hp * 8 + j // 2 + 1], lhsT=Kpg[:, j, hp * 128:(hp + 1) * 128],
                                         rhs=ones[:, 0:1], start=(j % 2 == 0), stop=(j % 2 == 1))
                    return r
                S.op("pe", kms, reads=["pA", "pB", "ones"], writes=["ps2"])
                kmS = sm[:, 0:16].rearrange("p (h n) -> p h n", h=2)
                S.op("dve", lambda e: e.tensor_copy(out=sm[:, 0:16], in_=PS[2][:, 0:16]), reads=["ps2"], writes=["acc"])
                S.op("dve", lambda e: e.tensor_copy(out=Vb16[:, :, :, 0:64],
                                                    in_=Vpg[:, :, :].rearrange("p j (h d) -> p j h d", h=4)),
                     reads=["pC", "pD"], writes=["Vb16"])
                qsel = [qsS[:, hp, :].rearrange("p (t s) -> p s t", s=NSQ)[:, s_, :] for hp in range(2)]
                ksel = [ksS[:, hp, :].rearrange("p (t s) -> p s t", s=NSQ)[:, s_, :] for hp in range(2)]
                Qf = sm[:, 16:48].rearrange("p (h c) -> p h c", h=2)
                Qb = sq[0][:, 0:16].bitcast(BF16).rearrange("p (h c) -> p h c", h=2)
                Kn = sq[0][:, 16:20].bitcast(BF16).rearrange("p (h c) -> p h c", h=2)
                for hp in range(2):
                    S.op("dve", lambda e, hp=hp, qs_=qsel[hp]: e.tensor_tensor(
                        out=Qf[:, hp, :].rearrange("p (h t) -> p h t", h=4),
                        in0=qs_.rearrange("p (o t) -> p o t", o=1).to_broadcast([128, 4, 4]),
                        in1=bm4[:, hp, :].rearrange("p (h o) -> p h o", o=1).to_broadcast([128, 4, 4]), op=ALU.mult),
                        reads=["qsS", "bm4", "acc"], writes=["acc"])
                    S.op("pool", lambda e, hp=hp, ks_=ksel[hp]: e.tensor_copy(out=Kn[:, hp, :], in_=ks_), reads=["ksS"], writes=["sq0"])
                S.op("act", lambda e: e.mul(out=Qb[:, :, :], in_=Qf[:, :, :], mul=0.125), reads=["acc"], writes=["sq0"])

                def gm(e):
                    e.matmul(PS[3][0:16, 0:8], lhsT=Qf[:, 0, :], rhs=kmS[:, 0, :], start=True, stop=False)
                    return e.matmul(PS[3][0:16, 0:8], lhsT=Qf[:, 1, :], rhs=kmS[:, 1, :], start=False, stop=True)
                S.op("pe", gm, reads=["acc"], writes=["ps3"])
                gt = sm[0:16, 64:128]
                S.op("dve", lambda e: e.tensor_copy(out=gt[:, 0:8], in_=PS[3][0:16, 0:8]), reads=["ps3"], writes=["acc"])
                S.op("dve", lambda e: e.max(out=gt[:, 8:16], in_=gt[:, 0:8]), reads=["acc"], writes=["acc"])
                S.op("dve", lambda e: e.tensor_scalar(out=gt[:, 16:24], in0=gt[:, 0:8], scalar1=gt[:, 10:11], scalar2=None,
                                                      op0=ALU.is_ge), reads=["acc"], writes=["acc"])
                S.op("dve", lambda e: e.tensor_scalar(out=gt[:, 24:32], in0=gt[:, 16:24], scalar1=-1.0, scalar2=1e30,
                                                      op0=ALU.add, op1=ALU.mult), reads=["acc"], writes=["acc"])
                S.op("pe", lambda e: e.transpose(out=PS[3][0:8, 16:32], in_=gt[:, 24:32], identity=ident[0:16, 0:16]),
                     reads=["acc", "ident"], writes=["ps3"])
                selT = sm[0:8, 128:144]
                S.op("act", lambda e: e.copy(out=selT, in_=PS[3][0:8, 16:32]), reads=["ps3"], writes=["acc"])

                def scm(e):
                    for kt in range(16):
                        o_ = PS[4][:, kt * 16:(kt + 1) * 16]
                        e.matmul(o_, lhsT=KT[:, 0, kt * 128:(kt + 1) * 128], rhs=Qb[:, 0, :], start=True, stop=False)
                        e.matmul(o_, lhsT=KT[:, 1, kt * 128:(kt + 1) * 128], rhs=Qb[:, 1, :], start=False, stop=False)
                        e.matmul(o_, lhsT=E8[0:8, kt // 2, :], rhs=selT, start=False, stop=True)
                    e.matmul(PS[5][0:4, 0:16], lhsT=Kn[:, 0, :], rhs=Qb[:, 0, :], start=True, stop=False)
                    return e.matmul(PS[5][0:4, 0:16], lhsT=Kn[:, 1, :], rhs=Qb[:, 1, :], start=False, stop=True)
                S.op("pe", scm, reads=["KT", "sq0", "E8", "acc"], writes=["ps4", "ps5"])
                Pb_ = sq[1][:, 0:128].bitcast(BF16)
                Pn = sq[1][0:4, 128:136].bitcast(BF16)
                S.op("act", lambda e: e.activation(out=Pb_, in_=PS[4][:, 0:256], func=AF.Exp), reads=["ps4"], writes=["sq1"])
                S.op("act", lambda e: e.activation(out=sm[0:4, 160:176], in_=PS[5][0:4, 0:16], func=AF.Exp), reads=["ps5"],
                     writes=["acc"])
                S.op("dve", lambda e: e.tensor_tensor(out=Pn.rearrange("p (h t) -> p h t", h=4),
                                                      in0=sm[0:4, 160:176].rearrange("p (h t) -> p h t", h=4),
                                                      in1=cm4[:, :, :], op=ALU.mult), reads=["acc", "cm4", "sq1"], writes=["sq1"])
                Vn = sq[1][0:4, 144:274].bitcast(BF16).rearrange("p (h d) -> p h d", h=4)

                def vnm(e, s_=s_):
                    for k in range(KC):
                        r = e.matmul(PS[5][0:4, 32:288], lhsT=xnT[:, k, TP:T].rearrange("p (t s) -> p s t", s=NSQ)[:, s_, :],
                                     rhs=wv[:, k, :], start=(k == 0), stop=(k == KC - 1))
                    return r
                S.op("pe", vnm, reads=["xn4", "wv"], writes=["ps5"])
                S.op("pool", lambda e: e.memset(Vn[:, :, 64:65], 1.0), reads=["sq1"], writes=["sq1"])
                S.op("dve", lambda e: e.tensor_copy(out=Vn[:, :, 0:64], in_=PS[5][0:4, 32:288].rearrange("p (h d) -> p h d", h=4)),
                     reads=["ps5", "sq1"], writes=["sq1"])

                def pvm(e):
                    for h in range(4):
                        o_ = PS[6][0:4, h * 65:(h + 1) * 65]
                        for kt in range(16):
                            e.matmul(o_, lhsT=Pb_[:, kt * 16 + h * 4:kt * 16 + h * 4 + 4], rhs=Vb16[:, kt, h, :],
                                     start=(kt == 0), stop=False)
                        r = e.matmul(o_, lhsT=Pn[:, h * 4:(h + 1) * 4], rhs=Vn[:, h, :], start=False, stop=True)
                    return r
                S.op("pe", pvm, reads=["sq1", "Vb16"], writes=["ps6"])
                O3 = PS[6][0:4, 0:260].rearrange("p (h d) -> p h d", h=4)
                S.op("dve", lambda e: e.reciprocal(out=sm[0:4, 192:196], in_=O3[:, :, 64]), reads=["ps6"], writes=["acc"])
                S.op("dve", lambda e: e.tensor_tensor(
                    out=sm[0:4, 200:456].rearrange("p (h d) -> p h d", h=4), in0=O3[:, :, 0:64],
                    in1=sm[0:4, 192:196].rearrange("p (h o) -> p h o", o=1).to_broadcast([4, 4, 64]), op=ALU.mult),
                    reads=["ps6", "acc"], writes=["acc"])

                def otm(e):
                    for hp in range(2):
                        r = e.transpose(out=PS[7][:, hp * 4:(hp + 1) * 4], in_=sm[0:4, 200 + hp * 128:200 + (hp + 1) * 128],
                                        identity=ident[0:4, 0:4])
                    return r
                S.op("pe", otm, reads=["acc", "ident"], writes=["ps7"])
                S.op("act", lambda e, s_=s_: e.copy(
                    out=mixT[:, 6:8, TP:T].rearrange("p c (t s) -> p c s t", s=NSQ)[:, :, s_, :],
                    in_=PS[7][:, 0:8].rearrange("p (c t) -> p c t", c=2)), reads=["ps7"], writes=["mix6s", "mix7s"])

        def wout_apply(l):
            S.dma("pool", lambda e: e.dma_start(out=woutb, in_=w_out[l].rearrange("(k p) n -> p k n", p=128)),
                  writes=["pA", "pB"])
            mk = [f"mix{c}" for c in range(8)] + [f"mix{c}s" for c in range(8)]
            for g, (t0, n) in enumerate(GROUPS):
                for nn in range(KC):
                    pb = 4 + (nn % 2)
                    po = PS[pb]

                    def om(e, nn=nn, t0=t0, n=n, po=po):
                        for k in range(KC):
                            r = e.matmul(po[:, :n], lhsT=woutb[:, k, nn * 128:(nn + 1) * 128],
                                         rhs=mixT[:, k, t0:t0 + n], start=(k == 0), stop=(k == KC - 1))
                        return r
                    S.op("pe", om, reads=["pA", "pB"] + mk, writes=[f"ps{pb}"])
                    S.op("dve", lambda e, nn=nn, t0=t0, n=n, po=po: e.tensor_tensor(
                        out=xT[:, nn, t0:t0 + n], in0=po[:, :n], in1=xT[:, nn, t0:t0 + n], op=ALU.add),
                        reads=[f"ps{pb}", f"xT{g}"], writes=[f"xT{g}"])


        def tokmajor_rows(l):
            wkv = av(WB + 1024, 2048, BF16, "p (k f) -> p k f", k=KC)
            S.dma("pool", lambda e: e.dma_start(out=wkv, in_=w_in[l][:, 3080:3592].rearrange("(k p) f -> p k f", p=128)),
                  writes=["pA"])
            for tt in range(17):
                n = 128 if tt < 16 else 64
                b = tt % 2
                pp = PS[b]

                def km(e, tt=tt, n=n, pp=pp):
                    for k in range(KC):
                        r = e.matmul(pp[0:n, :], lhsT=xnT[:, k, tt * 128:tt * 128 + n], rhs=wkv[:, k, :],
                                     start=(k == 0), stop=(k == KC - 1))
                    return r
                S.op("pe", km, reads=["pA", f"xn{tt // 4}"], writes=[f"ps{b}"])
                if tt < 16:
                    S.op("pool" if False else "dve", lambda e, tt=tt, pp=pp: e.tensor_copy(out=Vb[:, tt, :], in_=pp[:, 256:512]),
                         reads=[f"ps{b}"], writes=["pD"])
                if b == 0:
                    S.op("act", lambda e, n=n, pp=pp: e.copy(out=sg[0][0:n, :], in_=pp[0:n, :]),
                         reads=["ps0"], writes=["sg0"])
                else:
                    S.op("dve", lambda e, n=n, pp=pp: e.tensor_copy(out=sg[1][0:n, :], in_=pp[0:n, :]),
                         reads=["ps1"], writes=["sg1"])
                if tt < 16:
                    S.dma("sp", lambda e, tt=tt, b=b: e.dma_start(out=p_k[l][tt * 128:(tt + 1) * 128, :],
                                                                 in_=sg[b][:, 0:256]), reads=[f"sg{b}"])
                    S.dma("act", lambda e, tt=tt, b=b: e.dma_start(out=p_v[l][tt * 128:(tt + 1) * 128, :],
                                                                  in_=sg[b][:, 256:512]), reads=[f"sg{b}"])
                else:
                    S.dma("sp", lambda e, b=b: e.dma_start(out=s_k[l].rearrange("s t f -> t s f"),
                                                          in_=sg[b][0:64, 0:256]), reads=[f"sg{b}"])
                    S.dma("act", lambda e, b=b: e.dma_start(out=s_v[l].rearrange("s t f -> t s f"),
                                                           in_=sg[b][0:64, 256:512]), reads=[f"sg{b}"])
            wq = av(WB + 1024, 6144, BF16, "p (k f) -> p k f", k=KC)
            S.dma("pool", lambda e: e.dma_start(out=wq, in_=w_in[l][:, 0:1536].rearrange("(k p) f -> p k f", p=128)),
                  writes=["pA", "pB", "pC"])
            crp = [xtok[0][:, 0:512], xtok[0][:, 512:1024], xtok[1][:, 0:512]]
            crk = ["xtok0a", "xtok0b", "xtok1a"]
            for (t0, n, key) in ((TP - 3, 3, "xn3"), (TP, 64, "xn4")):
                for j in range(3):
                    pp = PS[j % 2]

                    def cm(e, t0=t0, n=n, j=j, pp=pp):
                        for k in range(KC):
                            r = e.matmul(pp[0:n, :], lhsT=xnT[:, k, t0:t0 + n], rhs=wq[:, k, j * 512:(j + 1) * 512],
                                         start=(k == 0), stop=(k == KC - 1))
                        return r
                    S.op("pe", cm, reads=["pA", "pB", "pC", key], writes=[f"ps{j % 2}"])
                    S.op("act", lambda e, n=n, j=j, pp=pp: e.copy(out=crp[j][0:n, :], in_=pp[0:n, :]),
                         reads=[f"ps{j % 2}"], writes=[crk[j]])
                for j in range(3):
                    if n == 3:
                        S.dma("sp", lambda e, j=j: e.dma_start(out=p_conv[l][:, j * 512:(j + 1) * 512], in_=crp[j][0:3, :]),
                              reads=[crk[j]])
                    else:
                        S.dma("sp", lambda e, j=j: e.dma_start(
                            out=s_conv[l][:, :, j * 512:(j + 1) * 512].rearrange("s t f -> t s f"), in_=crp[j][16:64, :]),
                            reads=[crk[j]])

        def mixer(l):
            S.barrier()
            rms_norm_all(3 * l + 1)
            tokmajor_rows(l)
            for c in (6, 7):
                S.op("pool", lambda e, c=c: e.memset(mixT[:, c, :], 0.0), writes=[f"mix{c}", f"mix{c}s"])
            moba_prompt(l)
            S.barrier()
            if stage == 1:
                raise StopIteration
            moba_sample(l)
            S.barrier()
            if stage == 3:
                raise StopIteration
            gdn_mixer(l)
            S.barrier()
            if stage == 2:
                raise StopIteration
            sconv_mixer(l)
            S.barrier()
            wout_apply(l)
            S.barrier()

        try:
            for l in range(DEPTH):
                ffn(l, 0, 3 * l + 0)
                mixer(l)
                ffn(l, 1, 3 * l + 2)
        except StopIteration:
            pass

        S.barrier()
        if stage != 99 and stage < 0:
            dbg = dout("dbg", [128, 18944])
            S.dma("sp", lambda e: e.dma_start(out=dbg[:, :], in_=arena[:, :]))
        yn = av(0, 4096, F32, "p (k t) -> p k t", k=KC)
        for g, (t0, n) in enumerate(GROUPS if stage == 99 else []):
            rms_stats(g, t0, n)
            for k in range(KC):
                S.op("dve", lambda e, k=k, t0=t0, n=n: e.scalar_tensor_tensor(
                    out=yn[:, k, :n], in0=xT[:, k, t0:t0 + n], scalar=gains[:, 6, k:k + 1],
                    in1=rstd[:, :n], op0=ALU.mult, op1=ALU.mult),
                    reads=[f"xT{g}", "rstd", "gains"], writes=["yn"])
            for tl in range((n + 127) // 128):
                m = min(128, n - tl * 128)
                b = tl % 2
                for half in range(2):
                    pt = PS[6 + half]

                    def tr2(e, tl=tl, m=m, half=half, pt=pt):
                        for j in range(4):
                            k = half * 4 + j
                            r = e.transpose(out=pt[0:m, j * 128:(j + 1) * 128], in_=yn[:, k, tl * 128:tl * 128 + m],
                                            identity=ident[:])
                        return r
                    S.op("pe", tr2, reads=["yn", "ident"], writes=[f"ps{6 + half}"])
                    if half == 0:
                        S.op("dve", lambda e, m=m, b=b, pt=pt: e.tensor_copy(out=xtok[b][0:m, 0:512], in_=pt[0:m, :]),
                             reads=["ps6"], writes=[f"xtok{b}a"])
                    else:
                        S.op("act", lambda e, m=m, b=b, pt=pt: e.copy(out=xtok[b][0:m, 512:1024], in_=pt[0:m, :]),
                             reads=["ps7"], writes=[f"xtok{b}b"])
                S.dma("sp", lambda e, t0=t0, tl=tl, m=m, b=b: e.dma_start(
                    out=y[t0 + tl * 128:t0 + tl * 128 + m, :], in_=xtok[b][0:m, :]),
                    reads=[f"xtok{b}a", f"xtok{b}b"])
        S.emit()
        print("sched stats", S.stats, "sbuf bytes left", nc.sbuf_bytes_remaining)
    return nc


_W_KEYS = ["ffn1_w_gate", "ffn1_w_up", "ffn1_w_down", "ffn2_w_gate", "ffn2_w_up", "ffn2_w_down",
           "w_in", "w_out", "sc_conv_w", "gdn_conv_w", "gdn_a_log", "gdn_dt_bias", "gdn_norm_w"]


def kernel(**inp):
    nc = build_nc()
    norms = np.ascontiguousarray(np.concatenate([
        np.stack([inp["norm_ffn1"][l], inp["norm_mix"][l], inp["norm_ffn2"][l]]) for l in range(DEPTH)]
        + [inp["norm_final"][None, :]], axis=0).astype(np.float32))
    in_maps = []
    for c in range(NCORES):
        sl = slice(c * NSQ, (c + 1) * NSQ)
        xs = inp["x_sample"][sl].transpose(1, 0, 2).reshape(TS, D)
        xin = np.concatenate([inp["x_prompt"][c], xs], axis=0)
        m = {"xin": np.ascontiguousarray(xin), "norms": norms,
             "state_sconv": np.ascontiguousarray(inp["state_sconv"][:, sl]),
             "state_gdn": np.ascontiguousarray(inp["state_gdn"][:, sl]),
             "state_gdn_conv": np.ascontiguousarray(inp["state_gdn_conv"][:, sl]),
             "cache_k": inp["cache_k"], "cache_v": inp["cache_v"],
             "page_table": np.ascontiguousarray(inp["page_table"][sl].reshape(1, NSQ * 16).astype(np.int32))}
        for k in _W_KEYS:
            m[k] = inp[k]
        in_maps.append(m)
    res = run_bass_kernel_spmd(nc, in_maps, core_ids=list(range(NCORES)))
    R = res.results

    def cat_p(name, shp):
        return np.ascontiguousarray(np.stack([r[name].reshape((DEPTH,) + shp) for r in R], axis=1))

    def cat_s(name, shp):
        return np.ascontiguousarray(np.concatenate([r[name].reshape((DEPTH, NSQ) + shp) for r in R], axis=1))
    y_prompt = np.stack([r["y"][:TP] for r in R], axis=0)
    y_sample = np.concatenate([r["y"][TP:].reshape(4, NSQ, D).transpose(1, 0, 2) for r in R], axis=0)
    return (y_prompt, np.ascontiguousarray(y_sample),
            cat_p("p_gdn", (4, 128, 128)), cat_p("p_conv", (3, 1536)), cat_p("p_sconv", (2, 256)),
            cat_p("p_k", (TP, 4, 64)), cat_p("p_v", (TP, 4, 64)),
            cat_s("s_gdn", (4, 128, 128)), cat_s("s_conv", (3, 1536)), cat_s("s_sconv", (2, 256)),
            cat_s("s_k", (4, 4, 64)), cat_s("s_v", (4, 4, 64)))
```

```python
import contextlib
import numpy as np
import concourse.bass as bass
import concourse.mybir as mybir
from concourse.bass_utils import run_bass_kernel_spmd

F32 = mybir.dt.float32
BF16 = mybir.dt.bfloat16
I32 = mybir.dt.int32
AF = mybir.ActivationFunctionType
ALU = mybir.AluOpType
AX = mybir.AxisListType

NCORES = 8
D = 1024
KC = 8
DFF = 2816
PW = 3592
TP = 2048
NSQ = 16
TS = 64
T = TP + TS
DEPTH = 2
EPS = 1e-6
GROUPS = [(0, 512), (512, 512), (1024, 512), (1536, 512), (2048, 64)]
PASSES = [(0, 6), (6, 6), (12, 5), (17, 5)]


class _Op:
    __slots__ = ("eng", "fn", "is_dma", "deps", "signal", "tick", "dsem", "dval")

    def __init__(self, eng, fn, is_dma):
        self.eng = eng
        self.fn = fn
        self.is_dma = is_dma
        self.deps = []
        self.signal = False
        self.tick = 0
        self.dsem = -1
        self.dval = 0


class Sched:
    def __init__(self, nc, n_dma_sems=64):
        self.nc = nc
        self.ops = []
        self.last_w = {}
        self.readers = {}
        self.n_dma_sems = n_dma_sems
        self.dma_count = 0
        self.sw_count = 0
        self.hw_count = 0
        self.dma_last = [None] * n_dma_sems
        self.dma_cnt_per = [0] * n_dma_sems
        self.last_on = {}
        self.engs = {"pe": nc.tensor, "act": nc.scalar, "dve": nc.vector, "pool": nc.gpsimd,
                     "sp": nc.sync}

    def _add_dep(self, op, tgt, raw):
        if tgt is None or tgt is op:
            return
        if not tgt.is_dma and not op.is_dma and tgt.eng == op.eng and op.eng == "pe":
            return
        op.deps.append(tgt)

    def _record(self, op, reads, writes):
        pr = [k for k in reads if k.startswith("ps")]
        if pr:
            writes = list(writes) + [k for k in pr if k not in writes]
        for k in reads:
            self._add_dep(op, self.last_w.get(k), True)
        for k in writes:
            self._add_dep(op, self.last_w.get(k), False)
            for r in self.readers.get(k, ()):
                self._add_dep(op, r, False)
        for k in reads:
            self.readers.setdefault(k, []).append(op)
        for k in writes:
            self.last_w[k] = op
            self.readers[k] = []
        self.ops.append(op)
        if not op.is_dma:
            self.last_on[op.eng] = op
        return op

    def op(self, eng, fn, reads=(), writes=()):
        return self._record(_Op(eng, fn, False), reads, writes)

    def dma(self, queue, fn, reads=(), writes=()):
        op = _Op(queue, fn, True)
        nsw = self.n_dma_sems // 3
        if queue == "pool":
            k = self.sw_count % nsw
            self.sw_count += 1
        else:
            k = nsw + self.hw_count % (self.n_dma_sems - nsw)
            self.hw_count += 1
        self.dma_count += 1
        op.dsem = k
        self.dma_cnt_per[k] += 1
        op.dval = 16 * self.dma_cnt_per[k]
        prev = self.dma_last[k]
        if prev is not None:
            op.deps.append(prev)
        self.dma_last[k] = op
        return self._record(op, reads, writes)

    def barrier(self):
        tg = [o for o in self.last_on.values()] + [d for d in self.dma_last if d is not None]
        for e in self.engs:
            b = _Op(e, None, False)
            b.deps = [t for t in tg]
            self.ops.append(b)
        self.last_w = {}
        self.readers = {}

    def emit(self, final_wait_eng="sp"):
        nc = self.nc
        fin = _Op(final_wait_eng, None, False)
        fin.deps = [d for d in self.dma_last if d is not None]
        self.ops.append(fin)
        for o in self.ops:
            for t in o.deps:
                if not t.is_dma:
                    t.signal = True
        cnt = {e: 0 for e in self.engs}
        for o in self.ops:
            if not o.is_dma and o.signal:
                cnt[o.eng] += 1
                o.tick = cnt[o.eng]
        with contextlib.ExitStack() as es:
            esem = {e: es.enter_context(nc.semaphore(f"s_{e}")) for e in self.engs}
            dsem = [es.enter_context(nc.semaphore(f"d_{i}")) for i in range(self.n_dma_sems)]
            waited = {e: {} for e in self.engs}
            n_wait = 0
            for o in self.ops:
                eng = self.engs[o.eng]
                w = waited[o.eng]
                need = {}
                for t in o.deps:
                    if t.is_dma:
                        key, val = ("d", t.dsem), t.dval
                    else:
                        key, val = ("e", t.eng), t.tick
                    if w.get(key, 0) >= val:
                        continue
                    if need.get(key, 0) < val:
                        need[key] = val
                for key, val in need.items():
                    sem = dsem[key[1]] if key[0] == "d" else esem[key[1]]
                    eng.wait_ge(sem, val)
                    w[key] = val
                    n_wait += 1
                if o.fn is None:
                    continue
                ins = o.fn(eng)
                if o.is_dma:
                    ins.then_inc(dsem[o.dsem], 16)
                elif o.signal:
                    ins.then_inc(esem[o.eng], 1)
            self.stats = dict(n_ops=len(self.ops), n_wait=n_wait, ticks=cnt, n_dma=self.dma_count)


def build_nc(stage=99):
    nc = bass.Bass("TRN2", target_bir_lowering=False)

    def din(name, shape, dt=F32):
        return nc.dram_tensor(name, list(shape), dt, kind="ExternalInput").ap()

    def dout(name, shape, dt=F32):
        return nc.dram_tensor(name, list(shape), dt, kind="ExternalOutput").ap()

    xin = din("xin", [T, D])
    norms = din("norms", [7, D])
    wg = [din("ffn1_w_gate", [DEPTH, D, DFF]), din("ffn2_w_gate", [DEPTH, D, DFF])]
    wu = [din("ffn1_w_up", [DEPTH, D, DFF]), din("ffn2_w_up", [DEPTH, D, DFF])]
    wd = [din("ffn1_w_down", [DEPTH, DFF, D]), din("ffn2_w_down", [DEPTH, DFF, D])]
    y = dout("y", [T, D])
    w_in = din("w_in", [DEPTH, D, PW])
    w_out = din("w_out", [DEPTH, D, D])
    sc_conv_w = din("sc_conv_w", [DEPTH, 3, 256])
    state_sconv = din("state_sconv", [DEPTH, NSQ, 2, 256])
    p_sconv = dout("p_sconv", [DEPTH, 2, 256])
    p_conv = dout("p_conv", [DEPTH, 3, 1536])
    s_conv = dout("s_conv", [DEPTH, NSQ, 3, 1536])
    p_k = dout("p_k", [DEPTH, TP, 256])
    p_v = dout("p_v", [DEPTH, TP, 256])
    s_k = dout("s_k", [DEPTH, NSQ, 4, 256])
    s_v = dout("s_v", [DEPTH, NSQ, 4, 256])
    p_gdn = dout("p_gdn", [DEPTH, 4, 128, 128])
    state_gdn = din("state_gdn", [DEPTH, NSQ, 4, 128, 128])
    cache_k = din("cache_k", [DEPTH, 2560, 128, 4, 64])
    cache_v = din("cache_v", [DEPTH, 2560, 128, 4, 64])
    page_table = din("page_table", [1, NSQ * 16], I32)
    state_gdn_conv = din("state_gdn_conv", [DEPTH, NSQ, 3, 1536])
    gdn_conv_w = din("gdn_conv_w", [DEPTH, 4, 1536])
    gdn_a_log = din("gdn_a_log", [DEPTH, 4])
    gdn_dt_bias = din("gdn_dt_bias", [DEPTH, 4])
    gdn_norm_w = din("gdn_norm_w", [DEPTH, 128])
    s_gdn = dout("s_gdn", [DEPTH, NSQ, 4, 128, 128])
    s_sconv = dout("s_sconv", [DEPTH, NSQ, 2, 256])

    with contextlib.ExitStack() as es:
        def sb(name, shape, dt=F32):
            return es.enter_context(nc.sbuf_tensor(name, list(shape), dt))

        def ps(name, shape, dt=F32):
            return es.enter_context(nc.psum_tensor(name, list(shape), dt))

        S = Sched(nc)
        xT = sb("xT", [128, KC, T])
        xnT = sb("xnT", [128, KC, T], BF16)
        ident = sb("ident", [128, 128])
        ones = sb("ones", [128, 128])
        epsb = sb("epsb", [128, 1])
        gains = sb("gains", [128, 7, KC])
        arena = sb("arena", [128, 18944])

        def av(off, nw, dt=F32, pat=None, **kw):
            v = arena[:, off:off + nw]
            if dt != F32:
                v = v.bitcast(dt)
            if pat is not None:
                v = v.rearrange(pat, **kw)
            return v
        hT = av(0, 6336, BF16, "p (a t) -> p a t", a=6)
        wgu = [av(6336 + 1024 * i, 1024, BF16, "p (j k f) -> p j k f", j=2, k=KC) for i in range(2)]
        wdb = [av(8384 + 3072 * i, 3072, BF16, "p (c n) -> p c n", c=6) for i in range(2)]
        mixT = av(0, 8448, BF16, "p (a t) -> p a t", a=8)
        WB = 8448
        wcol = [av(WB + 512 * i, 512, BF16, "p (k f) -> p k f", k=KC) for i in range(2)]
        pbuf = {n: av(WB + 1024 + 2112 * i, 2112) for i, n in enumerate("ABCD")}
        tmpw = av(WB + 9472, 1024)
        woutb = av(WB + 1024, 4096, BF16, "p (k n) -> p k n", k=KC)
        Vb = av(WB + 1024 + 3 * 2112, 2048, BF16, "p (t f) -> p t f", t=16)
        scw = sb("scw", [128, DEPTH, 2, 3])
        onecol = sb("onecol", [128, 1])
        sg = [sb(f"sg{i}", [128, 512]) for i in range(2)]
        sq = [sb(f"sq{i}", [128, 512]) for i in range(2)]
        acc = sb("acc", [128, 512])
        rstd = sb("rstd", [128, 512])
        xtok = [sb(f"xtok{i}", [128, D]) for i in range(2)]
        PS = [ps(f"ps{i}", [128, 512]) for i in range(8)]

        S.op("pool", lambda e: e.memset(ident[:], 1.0), writes=["ident"])

        def mk_ident(e):
            return e.affine_select(out=ident[:], in_=ident[:], pattern=[[-1, 128]],
                                   compare_op=ALU.is_equal, fill=0.0, base=0, channel_multiplier=1)
        S.op("pool", mk_ident, reads=["ident"], writes=["ident"])
        S.op("pool", lambda e: e.memset(ones[:], 1.0), writes=["ones"])
        S.op("pool", lambda e: e.memset(epsb[:], EPS), writes=["epsb"])
        S.op("pool", lambda e: e.memset(onecol[:], 1.0), writes=["onecol"])
        negm = sb("negm", [128, 8, 8])
        pastm = sb("pastm", [128, 8, 8])
        ownm = sb("ownm", [128, 8, 8])
        E8 = sb("E8", [8, 9, 128])
        triuf = sb("triuf", [128, 128])
        onesb = sb("onesb", [128, 128], BF16)
        kmT = sb("kmT", [128, 8])

        def mk_sel(t, inval, op, fill):
            def f(e):
                return e.affine_select(out=t[:], in_=t[:], pattern=[[1, 8], [-1, 8]], compare_op=op, fill=fill,
                                       base=0, channel_multiplier=0)
            return f
        S.op("pool", lambda e: e.memset(negm[:], 0.0), reads=["negm"], writes=["negm"])
        S.op("pool", mk_sel(negm, 0.0, ALU.is_gt, -1e30), reads=["negm"], writes=["negm"])
        S.op("pool", lambda e: e.memset(pastm[:], 1.0), writes=["pastm"])
        S.op("pool", mk_sel(pastm, 1.0, ALU.is_gt, 0.0), reads=["pastm"], writes=["pastm"])
        S.op("pool", lambda e: e.memset(ownm[:], 1.0), writes=["ownm"])
        S.op("pool", mk_sel(ownm, 1.0, ALU.is_equal, 0.0), reads=["ownm"], writes=["ownm"])

        S.op("pool", lambda e: e.memset(E8[:], 1.0), writes=["E8"])

        def mk_e8(e):
            return e.affine_select(out=E8[:], in_=E8[:], pattern=[[-1, 9], [0, 128]], compare_op=ALU.is_equal,
                                   fill=0.0, base=0, channel_multiplier=1)
        S.op("pool", mk_e8, reads=["E8"], writes=["E8"])

        S.op("pool", lambda e: e.memset(triuf[:], 1.0), writes=["triuf"])

        def mk_triu(e):
            return e.affine_select(out=triuf[:], in_=triuf[:], pattern=[[1, 128]], compare_op=ALU.is_ge, fill=0.0,
                                   base=0, channel_multiplier=-1)
        S.op("pool", mk_triu, reads=["triuf"], writes=["triuf"])
        S.op("pool", lambda e: e.memset(onesb[:], 1.0), writes=["onesb"])
        mNL = sb("mNL", [128, 128])
        mNU = sb("mNU", [128, 128])
        sL = sb("sL", [128, 128])
        sU = sb("sU", [128, 128])

        def mk_m(t, cm, pat, op, fill):
            return lambda e: e.affine_select(out=t[:], in_=t[:], pattern=[[pat, 128]], compare_op=op, fill=fill,
                                             base=0, channel_multiplier=cm)
        for (t_, nm_, inv_, cm_, pat_, op_, fill_) in ((mNL, "mNL", 0.0, 1, -1, ALU.is_ge, -1e30),
                                                     (mNU, "mNU", 0.0, -1, 1, ALU.is_ge, -1e30),
                                                     (sL, "sL", 1.0, 1, -1, ALU.is_gt, 0.0),
                                                     (sU, "sU", 1.0, -1, 1, ALU.is_gt, 0.0)):
            S.op("pool", lambda e, t_=t_, inv_=inv_: e.memset(t_[:], inv_), writes=[nm_])
            S.op("pool", mk_m(t_, cm_, pat_, op_, fill_), reads=[nm_], writes=[nm_])
        pidx = sb("pidx", [128, NSQ * 16], I32)
        pidx1 = sb("pidx1", [128, NSQ * 16], I32)
        piota = sb("piota", [128, 1], I32)
        bm4 = sb("bm4", [128, 2, 4])
        cm4 = sb("cm4", [4, 4, 4])
        S.dma("sp", lambda e: e.dma_start(out=pidx[:], in_=page_table[0:1, :].partition_broadcast(128)), writes=["pidx"])
        S.op("pool", lambda e: e.iota(out=piota[:], pattern=[[0, 1]], base=0, channel_multiplier=1), writes=["piota"])
        S.op("dve", lambda e: e.tensor_scalar(out=pidx[:], in0=pidx[:], scalar1=128, scalar2=piota[:, 0:1], op0=ALU.mult,
                                              op1=ALU.add), reads=["pidx", "piota"], writes=["pidx"])
        S.op("dve", lambda e: e.tensor_scalar(out=pidx1[:], in0=pidx[:], scalar1=2560 * 128, scalar2=None, op0=ALU.add),
             reads=["pidx"], writes=["pidx1"])
        S.op("pool", lambda e: e.memset(bm4[:], 1.0), writes=["bm4"])
        S.op("pool", lambda e: e.affine_select(out=bm4[:], in_=bm4[:], pattern=[[128, 2], [-64, 4]], compare_op=ALU.is_ge,
                                               fill=0.0, base=0, channel_multiplier=1), reads=["bm4"], writes=["bm4"])
        S.op("pool", lambda e: e.affine_select(out=bm4[:], in_=bm4[:], pattern=[[-128, 2], [64, 4]], compare_op=ALU.is_gt,
                                               fill=0.0, base=64, channel_multiplier=-1), reads=["bm4"], writes=["bm4"])
        S.op("pool", lambda e: e.memset(cm4[:], 1.0), writes=["cm4"])
        S.op("pool", lambda e: e.affine_select(out=cm4[:], in_=cm4[:], pattern=[[0, 4], [1, 4]], compare_op=ALU.is_ge,
                                               fill=0.0, base=0, channel_multiplier=-1), reads=["cm4"], writes=["cm4"])
        gcw = sb("gcw", [128, DEPTH, 12, 4])
        gnw = sb("gnw", [128, DEPTH])
        albt = sb("albt", [128, DEPTH, 4])
        dtbt = sb("dtbt", [128, DEPTH, 4])
        bg = sb("bg", [128, 16, 8])
        bgs = sb("bgs", [4, NSQ, 8])
        Sp = sb("Sp", [128, 128])
        Ssm = av(5280, 512, F32, "p (s v) -> p s v", s=4)
        cs = av(5792, 336, F32, "p (j s t) -> p j s t", j=3, s=NSQ)
        for l_ in range(DEPTH):
            for c_ in range(12):
                S.dma("sp", lambda e, l_=l_, c_=c_: e.dma_start(
                    out=gcw[:, l_, c_, :], in_=gdn_conv_w[l_][:, c_ * 128:(c_ + 1) * 128].rearrange("i p -> p i"),
                    allow_slow_non_contiguous=True), writes=["gcw"])
        S.dma("sp", lambda e: e.dma_start(out=gnw[:], in_=gdn_norm_w.rearrange("l p -> p l"),
                                          allow_slow_non_contiguous=True), writes=["gnw"])
        for l_ in range(DEPTH):
            S.dma("sp", lambda e, l_=l_: e.dma_start(out=albt[:, l_, :], in_=gdn_a_log[l_:l_ + 1, :].partition_broadcast(128)),
                  writes=["albt"])
            S.dma("sp", lambda e, l_=l_: e.dma_start(out=dtbt[:, l_, :], in_=gdn_dt_bias[l_:l_ + 1, :].partition_broadcast(128)),
                  writes=["dtbt"])
        S.op("act", lambda e: e.activation(out=albt[:], in_=albt[:], func=AF.Exp), reads=["albt"], writes=["albt"])
        S.op("dve", lambda e: e.tensor_scalar(out=albt[:], in0=albt[:], scalar1=-1.0, scalar2=None, op0=ALU.mult),
             reads=["albt"], writes=["albt"])
        S.dma("sp", lambda e: e.dma_start(out=gains[:], in_=norms.rearrange("n (k p) -> p n k", p=128),
                                          allow_slow_non_contiguous=True), writes=["gains"])

        for tt in range(17):
            n = 128 if tt < 16 else 64
            b = tt % 2
            S.dma("sp" if tt % 2 == 0 else "act",
                  lambda e, tt=tt, n=n, b=b: e.dma_start(out=xtok[b][0:n, :], in_=xin[tt * 128:tt * 128 + n, :]),
                  writes=[f"xtok{b}"])
            for half in range(2):
                pt = PS[6 + half]

                def tr(e, tt=tt, n=n, b=b, half=half, pt=pt):
                    for j in range(4):
                        k = half * 4 + j
                        r = e.transpose(out=pt[:, j * 128:j * 128 + n], in_=xtok[b][0:n, k * 128:(k + 1) * 128],
                                        identity=ident[0:n, 0:n])
                    return r
                S.op("pe", tr, reads=[f"xtok{b}", "ident"], writes=[f"ps{6 + half}"])
                S.op("dve" if half == 0 else "act",
                     (lambda e, tt=tt, n=n, half=half, pt=pt: e.tensor_copy(
                         out=xT[:, half * 4:half * 4 + 4, tt * 128:tt * 128 + n],
                         in_=pt[:].rearrange("p (j t) -> p j t", j=4)[:, :, 0:n])) if half == 0 else
                     (lambda e, tt=tt, n=n, half=half, pt=pt: e.copy(
                         out=xT[:, half * 4:half * 4 + 4, tt * 128:tt * 128 + n],
                         in_=pt[:].rearrange("p (j t) -> p j t", j=4)[:, :, 0:n])),
                     reads=[f"ps{6 + half}"], writes=[f"xT{tt // 4}"])

        def rms_stats(g, t0, n):
            for k in range(KC):
                if k == 0:
                    S.op("act", lambda e: e.activation(out=acc[:, :n], in_=xT[:, 0, t0:t0 + n], func=AF.Square),
                         reads=[f"xT{g}"], writes=["acc"])
                else:
                    b = k % 2
                    S.op("act", lambda e, k=k, b=b: e.activation(out=sq[b][:, :n], in_=xT[:, k, t0:t0 + n],
                                                                 func=AF.Square),
                         reads=[f"xT{g}"], writes=[f"sq{b}"])
                    S.op("pool", lambda e, b=b: e.tensor_tensor(out=acc[:, :n], in0=acc[:, :n], in1=sq[b][:, :n],
                                                                op=ALU.add),
                         reads=["acc", f"sq{b}"], writes=["acc"])
            S.op("pe", lambda e: e.matmul(PS[5][:, :n], lhsT=ones[:], rhs=acc[:, :n], start=True, stop=True),
                 reads=["ones", "acc"], writes=["ps5"])
            S.op("act", lambda e: e.activation(out=rstd[:, :n], in_=PS[5][:, :n], func=AF.Sqrt,
                                               bias=epsb[:, 0:1], scale=1.0 / D),
                 reads=["ps5", "epsb"], writes=["rstd"])
            S.op("dve", lambda e: e.reciprocal(out=rstd[:, :n], in_=rstd[:, :n]), reads=["rstd"], writes=["rstd"])

        def rms_norm_all(gi):
            for g, (t0, n) in enumerate(GROUPS):
                rms_stats(g, t0, n)
                for k in range(KC):
                    S.op("dve", lambda e, k=k, t0=t0, n=n: e.scalar_tensor_tensor(
                        out=xnT[:, k, t0:t0 + n], in0=xT[:, k, t0:t0 + n], scalar=gains[:, gi, k:k + 1],
                        in1=rstd[:, :n], op0=ALU.mult, op1=ALU.mult),
                        reads=[f"xT{g}", "rstd", "gains"], writes=[f"xn{g}"])

        wcount = [0]

        def ffn(l, which, gi):
            rms_norm_all(gi)
            WG, WU, WD = wg[which][l], wu[which][l], wd[which][l]
            for pi, (c0, ncp) in enumerate(PASSES):
                wb = wdb[pi % 2]
                S.dma("pool", lambda e, c0=c0, ncp=ncp, wb=wb: e.dma_start(
                    out=wb[:, 0:ncp, :], in_=WD[c0 * 128:(c0 + ncp) * 128, :].rearrange("(c p) n -> p c n", p=128)),
                    writes=[f"wdb{pi % 2}"])
                for ci in range(ncp):
                    c = c0 + ci
                    b = wcount[0] % 2
                    wcount[0] += 1
                    S.dma("pool", lambda e, c=c, b=b: e.dma_start(
                        out=wgu[b][:, 0, :, :], in_=WG[:, c * 128:(c + 1) * 128].rearrange("(k p) f -> p k f", p=128)),
                        writes=[f"wgu{b}g"])
                    S.dma("pool", lambda e, c=c, b=b: e.dma_start(
                        out=wgu[b][:, 1, :, :], in_=WU[:, c * 128:(c + 1) * 128].rearrange("(k p) f -> p k f", p=128)),
                        writes=[f"wgu{b}u"])
                    for g, (t0, n) in enumerate(GROUPS):
                        pb = g % 2
                        pg, pu = PS[pb], PS[2 + pb]

                        def gm(e, b=b, t0=t0, n=n, pg=pg, j=0):
                            for k in range(KC):
                                r = e.matmul(pg[:, :n], lhsT=wgu[b][:, j, k, :], rhs=xnT[:, k, t0:t0 + n],
                                             start=(k == 0), stop=(k == KC - 1))
                            return r
                        S.op("pe", gm, reads=[f"wgu{b}g", f"xn{g}"], writes=[f"ps{pb}"])
                        S.op("pe", lambda e, b=b, t0=t0, n=n, pu=pu: gm(e, b, t0, n, pu, 1),
                             reads=[f"wgu{b}u", f"xn{g}"], writes=[f"ps{2 + pb}"])
                        S.op("act", lambda e, pb=pb, n=n, pg=pg: e.activation(out=sg[pb][:, :n], in_=pg[:, :n],
                                                                             func=AF.Silu),
                             reads=[f"ps{pb}"], writes=[f"sg{pb}"])
                        S.op("dve", lambda e, pb=pb, n=n, pu=pu, ci=ci, t0=t0: e.tensor_tensor(
                            out=hT[:, ci, t0:t0 + n], in0=sg[pb][:, :n], in1=pu[:, :n], op=ALU.mult),
                            reads=[f"sg{pb}", f"ps{2 + pb}"], writes=[f"hT{ci}_{g}"])
                for g, (t0, n) in enumerate(GROUPS):
                    for nn in range(KC):
                        pb = 4 + (nn % 2)
                        po = PS[pb]

                        def dm(e, nn=nn, t0=t0, n=n, po=po, wb=wb, ncp=ncp):
                            for ci in range(ncp):
                                r = e.matmul(po[:, :n], lhsT=wb[:, ci, nn * 128:(nn + 1) * 128],
                                             rhs=hT[:, ci, t0:t0 + n], start=(ci == 0), stop=(ci == ncp - 1))
                            return r
                        S.op("pe", dm, reads=[f"wdb{pi % 2}"] + [f"hT{ci}_{g}" for ci in range(ncp)],
                             writes=[f"ps{pb}"])
                        S.op("dve", lambda e, nn=nn, t0=t0, n=n, po=po: e.scalar_tensor_tensor(
                            out=xT[:, nn, t0:t0 + n], in0=po[:, :n], scalar=0.5, in1=xT[:, nn, t0:t0 + n],
                            op0=ALU.mult, op1=ALU.add),
                            reads=[f"ps{pb}", f"xT{g}"], writes=[f"xT{g}"])


        for l_ in range(DEPTH):
            for c_ in range(2):
                S.dma("sp", lambda e, l_=l_, c_=c_: e.dma_start(
                    out=scw[:, l_, c_, :], in_=sc_conv_w[l_][:, c_ * 128:(c_ + 1) * 128].rearrange("i p -> p i"),
                    allow_slow_non_contiguous=True), writes=["scw"])

        def proj_fm(l, col0, ncols, dst, dkey, eng="act"):
            b = wcount[0] % 2
            wcount[0] += 1
            S.dma("pool", lambda e: e.dma_start(
                out=wcol[b][:, :, 0:ncols], in_=w_in[l][:, col0:col0 + ncols].rearrange("(k p) f -> p k f", p=128)),
                writes=[f"wc{b}"])
            for g, (t0, n) in enumerate(GROUPS):
                pb = g % 2
                pp = PS[pb]

                def pm(e, t0=t0, n=n, pp=pp):
                    for k in range(KC):
                        r = e.matmul(pp[0:ncols, :n], lhsT=wcol[b][:, k, 0:ncols], rhs=xnT[:, k, t0:t0 + n],
                                     start=(k == 0), stop=(k == KC - 1))
                    return r
                S.op("pe", pm, reads=[f"wc{b}", f"xn{g}"], writes=[f"ps{pb}"])
                if (g % 2 == 0) == (eng == "act"):
                    S.op("act", lambda e, t0=t0, n=n, pp=pp: e.copy(out=dst[0:ncols, t0:t0 + n], in_=pp[0:ncols, :n]),
                         reads=[f"ps{pb}"], writes=[dkey])
                else:
                    S.op("dve", lambda e, t0=t0, n=n, pp=pp: e.tensor_copy(out=dst[0:ncols, t0:t0 + n],
                                                                          in_=pp[0:ncols, :n]),
                         reads=[f"ps{pb}"], writes=[dkey])

        def sconv_mixer(l):
            pA, pB, pC, pD = pbuf["A"], pbuf["B"], pbuf["C"], pbuf["D"]
            ups = tmpw[:, 0:96].rearrange("p (s t) -> p s t", s=NSQ)
            ys = tmpw[:, 96:160].rearrange("p (s t) -> p s t", s=NSQ)
            for c in range(2):
                proj_fm(l, 2056 + c * 128, 128, pA, "pA")
                proj_fm(l, 2312 + c * 128, 128, pB, "pB", eng="dve")
                proj_fm(l, 2568 + c * 128, 128, pC, "pC")
                S.op("pool", lambda e: e.memset(pD[:, 0:2], 0.0), writes=["pD"])
                S.op("dve", lambda e: e.tensor_tensor(out=pD[:, 2:2 + TP], in0=pA[:, 0:TP], in1=pB[:, 0:TP],
                                                      op=ALU.mult), reads=["pA", "pB", "pD"], writes=["pD"])
                for t_ in range(2):
                    S.dma("sp", lambda e, c=c, t_=t_: e.dma_start(
                        out=ups[:, :, t_], in_=state_sconv[l][:, t_, c * 128:(c + 1) * 128].rearrange("s p -> p s"),
                        allow_slow_non_contiguous=True), writes=["ups"])
                S.op("pool", lambda e: e.tensor_tensor(
                    out=ups[:, :, 2:6], in0=pA[:, TP:T].rearrange("p (t s) -> p s t", s=NSQ),
                    in1=pB[:, TP:T].rearrange("p (t s) -> p s t", s=NSQ), op=ALU.mult),
                    reads=["pA", "pB", "ups"], writes=["ups"])
                S.dma("sp", lambda e, c=c: e.dma_start(
                    out=p_sconv[l][:, c * 128:(c + 1) * 128].rearrange("t p -> p t"), in_=pD[:, TP:TP + 2],
                    allow_slow_non_contiguous=True), reads=["pD"])
                for t_ in range(2):
                    S.dma("sp", lambda e, c=c, t_=t_: e.dma_start(
                        out=s_sconv[l][:, t_, c * 128:(c + 1) * 128].rearrange("s p -> p s"), in_=ups[:, :, 4 + t_],
                        allow_slow_non_contiguous=True), reads=["ups"])
                S.op("dve", lambda e, c=c: e.tensor_scalar(out=pA[:, 0:TP], in0=pD[:, 0:TP], scalar1=scw[:, l, c, 0:1],
                                                           scalar2=None, op0=ALU.mult),
                     reads=["pD", "scw"], writes=["pA"])
                for i in (1, 2):
                    S.op("dve", lambda e, c=c, i=i: e.scalar_tensor_tensor(
                        out=pA[:, 0:TP], in0=pD[:, i:i + TP], scalar=scw[:, l, c, i:i + 1], in1=pA[:, 0:TP],
                        op0=ALU.mult, op1=ALU.add), reads=["pD", "scw", "pA"], writes=["pA"])
                S.op("pool", lambda e, c=c: e.tensor_tensor(out=mixT[:, 4 + c, 0:TP], in0=pA[:, 0:TP], in1=pC[:, 0:TP],
                                                            op=ALU.mult), reads=["pA", "pC"], writes=[f"mix{4 + c}"])
                S.op("dve", lambda e, c=c: e.tensor_scalar(out=ys, in0=ups[:, :, 0:4], scalar1=scw[:, l, c, 0:1],
                                                           scalar2=None, op0=ALU.mult),
                     reads=["ups", "scw"], writes=["ys"])
                for i in (1, 2):
                    S.op("dve", lambda e, c=c, i=i: e.scalar_tensor_tensor(
                        out=ys, in0=ups[:, :, i:i + 4], scalar=scw[:, l, c, i:i + 1], in1=ys,
                        op0=ALU.mult, op1=ALU.add), reads=["ups", "scw", "ys"], writes=["ys"])
                S.op("dve", lambda e, c=c: e.tensor_tensor(
                    out=mixT[:, 4 + c, TP:T].rearrange("p (t s) -> p s t", s=NSQ), in0=ys,
                    in1=pC[:, TP:T].rearrange("p (t s) -> p s t", s=NSQ), op=ALU.mult),
                    reads=["ys", "pC"], writes=[f"mix{4 + c}s"])


        def gdn_gates(l):
            b = wcount[0] % 2
            wcount[0] += 1
            S.dma("pool", lambda e: e.dma_start(out=wcol[b][:, :, 0:8],
                                                in_=w_in[l][:, 2048:2056].rearrange("(k p) f -> p k f", p=128)),
                  writes=[f"wc{b}"])

            def gate_math(src, n, dst, nl, tkey, dkey):
                tm = tmpw[0:n, 0:nl * 8].rearrange("p (s f) -> p s f", s=nl)
                al = albt[0:n, l, :].rearrange("p (o f) -> p o f", o=1).to_broadcast([n, nl, 4])
                dt = dtbt[0:n, l, :].rearrange("p (o f) -> p o f", o=1).to_broadcast([n, nl, 4])
                S.op("act", lambda e: e.activation(out=dst[:, :, 0:4], in_=src[:, :, 0:4], func=AF.Sigmoid),
                     reads=[tkey], writes=[dkey])
                S.op("dve", lambda e: e.tensor_tensor(out=tm[:, :, 0:4], in0=src[:, :, 4:8], in1=dt, op=ALU.add),
                     reads=[tkey, "dtbt"], writes=["tmpw"])
                S.op("dve", lambda e: e.tensor_scalar(out=tm[:, :, 4:8], in0=tm[:, :, 0:4], scalar1=-1.0, scalar2=None,
                                                      op0=ALU.mult), reads=["tmpw"], writes=["tmpw"])
                S.op("dve", lambda e: e.tensor_tensor(out=tm[:, :, 4:8], in0=tm[:, :, 0:4], in1=tm[:, :, 4:8], op=ALU.max),
                     reads=["tmpw"], writes=["tmpw"])
                S.op("act", lambda e: e.activation(out=tm[:, :, 4:8], in_=tm[:, :, 4:8], func=AF.Exp, scale=-1.0),
                     reads=["tmpw"], writes=["tmpw"])
                S.op("act", lambda e: e.activation(out=tm[:, :, 4:8], in_=tm[:, :, 4:8], func=AF.Ln, bias=onecol[0:n, 0:1],
                                                   scale=1.0), reads=["tmpw", "onecol"], writes=["tmpw"])
                S.op("dve", lambda e: e.scalar_tensor_tensor(out=tm[:, :, 0:4], in0=tm[:, :, 0:4], scalar=0.0,
                                                             in1=tm[:, :, 4:8], op0=ALU.max, op1=ALU.add),
                     reads=["tmpw"], writes=["tmpw"])
                S.op("dve", lambda e: e.tensor_tensor(out=dst[:, :, 4:8], in0=tm[:, :, 0:4], in1=al, op=ALU.mult),
                     reads=["tmpw", "albt"], writes=[dkey])
            for tt in range(16):
                def bm(e, tt=tt):
                    for k in range(KC):
                        r = e.matmul(PS[0][:, 0:8], lhsT=xnT[:, k, tt * 128:(tt + 1) * 128], rhs=wcol[b][:, k, 0:8],
                                     start=(k == 0), stop=(k == KC - 1))
                    return r
                S.op("pe", bm, reads=[f"wc{b}", f"xn{tt // 4}"], writes=["ps0"])
                gate_math(PS[0][:, 0:8].rearrange("p (s f) -> p s f", s=1), 128,
                          bg[:, tt, :].rearrange("p (s f) -> p s f", s=1), 1, "ps0", "bg")

            def bms(e):
                xs = [xnT[:, k, TP:T].rearrange("p (t s) -> p s t", s=NSQ) for k in range(KC)]
                for s_ in range(NSQ):
                    for k in range(KC):
                        r = e.matmul(PS[1][0:4, s_ * 8:(s_ + 1) * 8], lhsT=xs[k][:, s_, :], rhs=wcol[b][:, k, 0:8],
                                     start=(k == 0), stop=(k == KC - 1))
                return r
            S.op("pe", bms, reads=[f"wc{b}", "xn4"], writes=["ps1"])
            gate_math(PS[1][0:4, 0:NSQ * 8].rearrange("p (s f) -> p s f", s=NSQ), 4, bgs[:, :, :], NSQ, "ps1", "bgs")

        def l2norm(buf, bkey, scale):
            for g, (t0, n) in enumerate(GROUPS):
                S.op("act", lambda e, t0=t0, n=n: e.activation(out=acc[:, :n], in_=buf[:, t0:t0 + n], func=AF.Square),
                     reads=[bkey], writes=["acc"])
                S.op("pe", lambda e, n=n: e.matmul(PS[5][:, :n], lhsT=ones[:], rhs=acc[:, :n], start=True, stop=True),
                     reads=["ones", "acc"], writes=["ps5"])
                S.op("act", lambda e, n=n: e.activation(out=rstd[:, :n], in_=PS[5][:, :n], func=AF.Sqrt,
                                                       bias=epsb[:, 0:1], scale=1.0), reads=["ps5", "epsb"],
                     writes=["rstd"])
                S.op("dve", lambda e, n=n: e.reciprocal(out=rstd[:, :n], in_=rstd[:, :n]), reads=["rstd"],
                     writes=["rstd"])
                S.op("dve", lambda e, t0=t0, n=n: e.scalar_tensor_tensor(
                    out=buf[:, t0:t0 + n], in0=buf[:, t0:t0 + n], scalar=scale, in1=rstd[:, :n], op0=ALU.mult,
                    op1=ALU.mult), reads=[bkey, "rstd"], writes=[bkey])

        def gdn_conv(l, h, j, buf, bkey):
            c = j * 4 + h
            w = lambda i: gcw[:, l, c, i:i + 1]
            csj = cs[:, j, :, :]
            for t_ in range(3):
                S.dma("sp", lambda e, t_=t_: e.dma_start(
                    out=csj[:, :, t_], in_=state_gdn_conv[l][:, t_, c * 128:(c + 1) * 128].rearrange("s p -> p s"),
                    allow_slow_non_contiguous=True), writes=[f"cs{j}"])
            xs = buf[:, TP:T].rearrange("p (t s) -> p s t", s=NSQ)
            S.op("pool", lambda e: e.tensor_copy(out=csj[:, :, 3:7], in_=xs), reads=[bkey, f"cs{j}"], writes=[f"cs{j}"])
            ys = tmpw[:, 512 + j * 64:512 + (j + 1) * 64].rearrange("p (s t) -> p s t", s=NSQ)
            S.op("dve", lambda e: e.tensor_scalar(out=ys, in0=csj[:, :, 3:7], scalar1=w(3), scalar2=None, op0=ALU.mult),
                 reads=[f"cs{j}", "gcw"], writes=[f"ys{j}"])
            for i in range(3):
                S.op("dve", lambda e, i=i: e.scalar_tensor_tensor(out=ys, in0=csj[:, :, i:i + 4], scalar=w(i), in1=ys,
                                                                  op0=ALU.mult, op1=ALU.add),
                     reads=[f"cs{j}", "gcw", f"ys{j}"], writes=[f"ys{j}"])
            S.op("act", lambda e: e.activation(out=xs, in_=ys, func=AF.Silu), reads=[f"ys{j}"], writes=[bkey])
            for blk in (3, 2, 1, 0):
                t0 = blk * 512
                tb = sg[blk % 2]
                S.op("dve", lambda e, t0=t0, tb=tb: e.tensor_scalar(out=tb[:, :], in0=buf[:, t0:t0 + 512], scalar1=w(3),
                                                                    scalar2=None, op0=ALU.mult),
                     reads=[bkey, "gcw"], writes=[f"sg{blk % 2}"])
                for i in range(3):
                    sh = 3 - i
                    lo = sh if blk == 0 else 0
                    S.op("dve", lambda e, t0=t0, tb=tb, i=i, sh=sh, lo=lo: e.scalar_tensor_tensor(
                        out=tb[:, lo:512], in0=buf[:, t0 + lo - sh:t0 + 512 - sh], scalar=w(i), in1=tb[:, lo:512],
                        op0=ALU.mult, op1=ALU.add), reads=[bkey, "gcw", f"sg{blk % 2}"], writes=[f"sg{blk % 2}"])
                S.op("act", lambda e, t0=t0, tb=tb: e.activation(out=buf[:, t0:t0 + 512], in_=tb[:, :], func=AF.Silu),
                     reads=[f"sg{blk % 2}"], writes=[bkey])

        TS_ = [sg[0], sg[1], sq[0], sq[1], acc, rstd, xtok[0][:, 0:512], xtok[0][:, 512:1024],
               xtok[1][:, 0:512], xtok[1][:, 512:1024], None, None, av(WB, 512), av(WB + 512, 512)]

        def gdn_chunk(C, NL, qT, kT, vT, Sap, beta, g, oT, skey, okey, LG=None, pre_lg=None, post_lg=None):
            W = NL * C
            LG = LG or NL
            Tt = list(TS_)
            Tt[10] = av(4224, 512)
            Tt[11] = av(4736, 512)
            tk = ["sg0", "sg1", "sq0", "sq1", "acc", "rstd", "xtok0a", "xtok0b", "xtok1a", "xtok1b", "scrA", "scrB", "wc0", "wc1"]
            SM = Tt[0]
            gc, egc, nbeta, wt = (SM[0:C, i * NL:(i + 1) * NL] for i in range(4))
            v3 = lambda t, p=C: t[0:p, 0:W].rearrange("p (l c) -> p l c", l=NL)
            vk = lambda t: t[0:C, 0:NL * 128].rearrange("p (l d) -> p l d", l=NL)
            bc_l = lambda col: col.rearrange("p (l o) -> p l o", o=1).to_broadcast([C, NL, C])
            bc_m = lambda m: m[0:C, 0:C].rearrange("p (o c) -> p o c", o=1).to_broadcast([C, NL, C])
            rd = ["pA", "pB", "pC"]
            S.op("pe", lambda e: e.matmul(PS[0][0:C, 0:NL], lhsT=triuf[0:C, 0:C], rhs=g, start=True, stop=True),
                 reads=["triuf", "bg", "bgs"], writes=["ps0"])
            S.op("dve", lambda e: e.tensor_copy(out=gc, in_=PS[0][0:C, 0:NL]), reads=["ps0"], writes=[tk[0]])
            S.op("act", lambda e: e.activation(out=egc, in_=PS[0][0:C, 0:NL], func=AF.Exp), reads=["ps0"], writes=[tk[0]])
            S.op("dve", lambda e: e.tensor_scalar(out=nbeta, in0=beta, scalar1=-1.0, scalar2=None, op0=ALU.mult),
                 reads=["bg", "bgs"], writes=[tk[0]])
            S.op("dve", lambda e: e.tensor_tensor(out=v3(Tt[1]), in0=bc_m(ident), in1=bc_l(gc), op=ALU.mult),
                 reads=["ident", tk[0]], writes=[tk[1]])
            S.op("pe", lambda e: e.matmul(PS[1][:, 0:W], lhsT=ones[0:C, :], rhs=Tt[1][0:C, 0:W], start=True, stop=True),
                 reads=["ones", tk[1]], writes=["ps1"])
            S.op("dve", lambda e: e.tensor_copy(out=Tt[2][:, 0:W], in_=PS[1][:, 0:W]), reads=["ps1"], writes=[tk[2]])
            S.op("act", lambda e: e.activation(out=Tt[8][:, 0:W], in_=PS[1][:, 0:W], func=AF.Exp), reads=["ps1"],
                 writes=[tk[8]])
            S.op("dve", lambda e: e.tensor_tensor(out=v3(Tt[1]), in0=bc_m(ident), in1=bc_l(beta), op=ALU.mult),
                 reads=["ident", "bg", "bgs", tk[1]], writes=[tk[1]])
            S.op("pe", lambda e: e.matmul(PS[2][0:C, 0:W], lhsT=ones[0:C, 0:C], rhs=Tt[1][0:C, 0:W], start=True, stop=True),
                 reads=["ones", tk[1]], writes=["ps2"])
            S.op("act", lambda e: e.copy(out=Tt[3][0:C, 0:W], in_=PS[2][0:C, 0:W]), reads=["ps2"], writes=[tk[3]])
            S.op("dve", lambda e: e.tensor_tensor(out=wt, in0=v3(Tt[2])[:, :, C - 1], in1=gc, op=ALU.subtract),
                 reads=[tk[2], tk[0]], writes=[tk[0]])
            S.op("act", lambda e: e.activation(out=wt, in_=wt, func=AF.Exp), reads=[tk[0]], writes=[tk[0]])

            def kkm(e):
                for ln in range(NL):
                    e.matmul(PS[3][0:C, ln * C:(ln + 1) * C], lhsT=kT(ln), rhs=kT(ln), start=True, stop=True)
                    r = e.matmul(PS[4][0:C, ln * C:(ln + 1) * C], lhsT=kT(ln), rhs=qT(ln), start=True, stop=True)
                return r
            S.op("pe", kkm, reads=rd, writes=["ps3", "ps4"])
            S.op("dve", lambda e: e.tensor_tensor(out=v3(Tt[4]), in0=bc_l(gc), in1=v3(Tt[2]), op=ALU.subtract),
                 reads=[tk[0], tk[2]], writes=[tk[4]])
            S.op("pool", lambda e: e.tensor_tensor(out=v3(Tt[4]), in0=v3(Tt[4]), in1=bc_m(mNL), op=ALU.add),
                 reads=[tk[4], "mNL"], writes=[tk[4]])
            S.op("act", lambda e: e.activation(out=v3(Tt[4]), in_=v3(Tt[4]), func=AF.Exp), reads=[tk[4]], writes=[tk[4]])
            S.op("pool", lambda e: e.tensor_tensor(out=v3(Tt[4]), in0=v3(Tt[4]), in1=bc_m(sL), op=ALU.mult),
                 reads=[tk[4], "sL"], writes=[tk[4]])
            S.op("dve", lambda e: e.tensor_tensor(out=v3(Tt[1]), in0=PS[3][0:C, 0:W].rearrange("p (l c) -> p l c", l=NL),
                                                  in1=bc_l(beta), op=ALU.mult),
                 reads=["ps3", "bg", "bgs", tk[1]], writes=[tk[1]])
            S.op("pool", lambda e: e.tensor_tensor(out=v3(Tt[4]), in0=v3(Tt[4]), in1=v3(Tt[1]), op=ALU.mult),
                 reads=[tk[4], tk[1]], writes=[tk[4]])
            S.op("dve", lambda e: e.tensor_tensor(out=v3(Tt[5]), in0=v3(Tt[2]), in1=bc_l(gc), op=ALU.subtract),
                 reads=[tk[0], tk[2]], writes=[tk[5]])
            S.op("pool", lambda e: e.tensor_tensor(out=v3(Tt[5]), in0=v3(Tt[5]), in1=bc_m(mNU), op=ALU.add),
                 reads=[tk[5], "mNU"], writes=[tk[5]])
            S.op("act", lambda e: e.activation(out=v3(Tt[5]), in_=v3(Tt[5]), func=AF.Exp), reads=[tk[5]], writes=[tk[5]])
            S.op("dve", lambda e: e.tensor_tensor(out=v3(Tt[6]), in0=PS[4][0:C, 0:W].rearrange("p (l c) -> p l c", l=NL),
                                                  in1=v3(Tt[5]), op=ALU.mult), reads=["ps4", tk[5]], writes=[tk[6]])
            S.op("pool", lambda e: e.tensor_tensor(out=v3(Tt[5]), in0=v3(Tt[5]), in1=bc_m(sU), op=ALU.mult),
                 reads=[tk[5], "sU"], writes=[tk[5]])
            S.op("dve", lambda e: e.tensor_tensor(out=v3(Tt[1]), in0=PS[3][0:C, 0:W].rearrange("p (l c) -> p l c", l=NL),
                                                  in1=v3(Tt[3]), op=ALU.mult), reads=["ps3", tk[3], tk[1]], writes=[tk[1]])
            S.op("pool", lambda e: e.tensor_tensor(out=v3(Tt[5]), in0=v3(Tt[5]), in1=v3(Tt[1]), op=ALU.mult),
                 reads=[tk[5], tk[1]], writes=[tk[5]])
            S.op("dve", lambda e: e.tensor_tensor(out=v3(Tt[7]), in0=bc_m(ident), in1=v3(Tt[5]), op=ALU.subtract),
                 reads=["ident", tk[5]], writes=[tk[7]])
            nlev = max(1, int(np.ceil(np.log2(C))) - 1)
            P, PT, Pn, PTn = 4, 5, 1, 3
            for lev in range(nlev):
                last = lev == nlev - 1

                def sqm(e, P=P, PT=PT, last=last):
                    for ln in range(NL):
                        sl = slice(ln * C, (ln + 1) * C)
                        r = e.matmul(PS[5][0:C, sl], lhsT=Tt[PT][0:C, sl], rhs=Tt[P][0:C, sl], start=True, stop=True)
                        if not last:
                            r = e.matmul(PS[6][0:C, sl], lhsT=Tt[P][0:C, sl], rhs=Tt[PT][0:C, sl], start=True, stop=True)
                    return r
                S.op("pe", sqm, reads=[tk[P], tk[PT]], writes=["ps5"] + ([] if last else ["ps6"]))
                S.op("act", lambda e, Pn=Pn: e.copy(out=Tt[Pn][0:C, 0:W], in_=PS[5][0:C, 0:W]), reads=["ps5"],
                     writes=[tk[Pn]])
                if not last:
                    S.op("dve", lambda e, PTn=PTn: e.tensor_copy(out=Tt[PTn][0:C, 0:W], in_=PS[6][0:C, 0:W]),
                         reads=["ps6"], writes=[tk[PTn]])

                def ym(e, Pn=Pn):
                    for ln in range(NL):
                        sl = slice(ln * C, (ln + 1) * C)
                        r = e.matmul(PS[7][0:C, sl], lhsT=Tt[Pn][0:C, sl], rhs=Tt[7][0:C, sl], start=True, stop=True)
                    return r
                S.op("pe", ym, reads=[tk[Pn], tk[7]], writes=["ps7"])
                S.op("dve", lambda e: e.tensor_tensor(out=Tt[7][0:C, 0:W], in0=Tt[7][0:C, 0:W], in1=PS[7][0:C, 0:W],
                                                      op=ALU.add), reads=[tk[7], "ps7"], writes=[tk[7]])
                P, PT, Pn, PTn = Pn, PTn, P, PT
            for ln in range(NL):
                S.op("dve", lambda e, ln=ln: e.tensor_tensor(out=Tt[9][:, ln * C:(ln + 1) * C], in0=qT(ln),
                                                             in1=Tt[8][:, ln * C:(ln + 1) * C], op=ALU.mult),
                     reads=["pA", tk[8]], writes=[tk[9]])

            vkg = lambda t: t[0:C, 0:LG * 128].rearrange("p (l d) -> p l d", l=LG)
            for lg in range(0, NL, LG):
                if pre_lg is not None:
                    pre_lg(lg)

                def trk(e, lg=lg):
                    for li in range(LG):
                        ln = lg + li
                        e.transpose(out=PS[0][0:C, li * 128:(li + 1) * 128], in_=kT(ln), identity=ident[:])
                        r = e.transpose(out=PS[1][0:C, li * 128:(li + 1) * 128], in_=vT(ln), identity=ident[:])
                    return r
                S.op("pe", trk, reads=rd + ["ident"], writes=["ps0", "ps1"])
                S.op("dve", lambda e, lg=lg: e.tensor_tensor(
                    out=vkg(Tt[10]), in0=PS[0][0:C, 0:LG * 128].rearrange("p (l d) -> p l d", l=LG),
                    in1=wt[:, lg:lg + LG].rearrange("p (l o) -> p l o", o=1).to_broadcast([C, LG, 128]), op=ALU.mult),
                    reads=["ps0", tk[0]], writes=[tk[10]])
                S.op("act", lambda e: e.copy(out=Tt[11][0:C, 0:LG * 128], in_=PS[1][0:C, 0:LG * 128]), reads=["ps1"],
                     writes=[tk[11]])
                for li in range(LG):
                    ln = lg + li
                    sl = slice(ln * C, (ln + 1) * C)
                    dl = slice(li * 128, (li + 1) * 128)
                    Sl = Sap(ln)
                    S.op("pe", lambda e, ln=ln, Sl=Sl: e.matmul(PS[2][0:C, 0:128], lhsT=kT(ln), rhs=Sl, start=True, stop=True),
                         reads=["pB", skey], writes=["ps2"])
                    S.op("dve", lambda e, ln=ln, dl=dl: e.scalar_tensor_tensor(
                        out=Tt[12][0:C, dl], in0=PS[2][0:C, 0:128], scalar=egc[:, ln:ln + 1], in1=Tt[11][0:C, dl],
                        op0=ALU.mult, op1=ALU.subtract), reads=["ps2", tk[0], tk[11]], writes=[tk[12]])
                    S.op("dve", lambda e, ln=ln, dl=dl: e.tensor_scalar(out=Tt[12][0:C, dl], in0=Tt[12][0:C, dl],
                                                                        scalar1=nbeta[:, ln:ln + 1], scalar2=None,
                                                                        op0=ALU.mult), reads=[tk[12], tk[0]], writes=[tk[12]])
                    S.op("pe", lambda e, sl=sl, dl=dl: e.matmul(PS[3][0:C, 0:128], lhsT=Tt[7][0:C, sl], rhs=Tt[12][0:C, dl],
                                                              start=True, stop=True), reads=[tk[7], tk[12]], writes=["ps3"])
                    S.op("act", lambda e, dl=dl: e.copy(out=Tt[13][0:C, dl], in_=PS[3][0:C, 0:128]), reads=["ps3"],
                         writes=[tk[13]])

                    def om(e, sl=sl, dl=dl, Sl=Sl):
                        e.matmul(PS[4][:, 0:C], lhsT=Sl, rhs=Tt[9][:, sl], start=True, stop=False)
                        return e.matmul(PS[4][:, 0:C], lhsT=Tt[13][0:C, dl], rhs=Tt[6][0:C, sl], start=False, stop=True)
                    S.op("pe", om, reads=[skey, tk[9], tk[13], tk[6]], writes=["ps4"])
                    S.op("act", lambda e, ln=ln: e.copy(out=oT(ln), in_=PS[4][:, 0:C]), reads=["ps4"], writes=[okey])
                    S.op("pe", lambda e, dl=dl: e.matmul(PS[5][:, 0:128], lhsT=Tt[10][0:C, dl], rhs=Tt[13][0:C, dl],
                                                        start=True, stop=True), reads=[tk[10], tk[13]], writes=["ps5"])
                    S.op("dve", lambda e, ln=ln, Sl=Sl: e.scalar_tensor_tensor(
                        out=Sl, in0=Sl, scalar=Tt[8][:, ln * C + C - 1:ln * C + C], in1=PS[5][:, 0:128], op0=ALU.mult,
                        op1=ALU.add), reads=[skey, tk[8], "ps5"], writes=[skey])
                if post_lg is not None:
                    post_lg(lg)

        def gdn_mixer(l):
            pA, pB, pC, pD = pbuf["A"], pbuf["B"], pbuf["C"], pbuf["D"]
            gdn_gates(l)
            for h in range(4):
                proj_fm(l, h * 128, 128, pA, "pA")
                proj_fm(l, 512 + h * 128, 128, pB, "pB", eng="dve")
                proj_fm(l, 1024 + h * 128, 128, pC, "pC")
                proj_fm(l, 1536 + h * 128, 128, pD, "pD", eng="dve")
                for j, (buf, key) in enumerate(((pA, "pA"), (pB, "pB"), (pC, "pC"))):
                    gdn_conv(l, h, j, buf, key)
                l2norm(pA, "pA", 128 ** -0.5)
                l2norm(pB, "pB", 1.0)
                S.op("act", lambda e: e.activation(out=pD[:, :], in_=pD[:, :], func=AF.Silu), reads=["pD"], writes=["pD"])
                S.op("pool", lambda e: e.memset(Sp[:], 0.0), writes=["Sp"])
                for n0 in range(0, 16, 4):
                    cl = lambda ln, n0=n0: slice((n0 + ln) * 128, (n0 + ln + 1) * 128)
                    gdn_chunk(128, 4, lambda ln, cl=cl: pA[:, cl(ln)], lambda ln, cl=cl: pB[:, cl(ln)],
                              lambda ln, cl=cl: pC[:, cl(ln)], lambda ln: Sp[:, :], bg[:, n0:n0 + 4, h],
                              bg[:, n0:n0 + 4, 4 + h], lambda ln, cl=cl: pA[:, cl(ln)], "Sp", "pA")
                S.dma("sp", lambda e, h=h: e.dma_start(out=p_gdn[l][h], in_=Sp[:, :]), reads=["Sp"])
                smp = lambda buf: buf[:, TP:T].rearrange("p (t s) -> p s t", s=NSQ)
                def st_in(lg, h=h):
                    S.dma("act", lambda e: e.dma_start(
                        out=Ssm[:, :, :], in_=state_gdn[l][lg:lg + 4, h].rearrange("s k v -> k s v")), writes=["Ssm"])

                def st_out(lg, h=h):
                    S.dma("act", lambda e: e.dma_start(
                        out=s_gdn[l][lg:lg + 4, h].rearrange("s k v -> k s v"), in_=Ssm[:, :, :]), reads=["Ssm"])
                gdn_chunk(4, NSQ, lambda ln: smp(pA)[:, ln, :], lambda ln: smp(pB)[:, ln, :], lambda ln: smp(pC)[:, ln, :],
                          lambda ln: Ssm[:, ln % 4, :], bgs[:, :, h], bgs[:, :, 4 + h], lambda ln: smp(pA)[:, ln, :],
                          "Ssm", "pA", LG=4, pre_lg=st_in, post_lg=st_out)
                for g, (t0, n) in enumerate(GROUPS):
                    S.op("act", lambda e, t0=t0, n=n: e.activation(out=acc[:, :n], in_=pA[:, t0:t0 + n], func=AF.Square),
                         reads=["pA"], writes=["acc"])
                    S.op("pe", lambda e, n=n: e.matmul(PS[5][:, :n], lhsT=ones[:], rhs=acc[:, :n], start=True, stop=True),
                         reads=["ones", "acc"], writes=["ps5"])
                    S.op("act", lambda e, n=n: e.activation(out=rstd[:, :n], in_=PS[5][:, :n], func=AF.Sqrt,
                                                           bias=epsb[:, 0:1], scale=1.0 / 128), reads=["ps5", "epsb"],
                         writes=["rstd"])
                    S.op("dve", lambda e, n=n: e.reciprocal(out=rstd[:, :n], in_=rstd[:, :n]), reads=["rstd"], writes=["rstd"])
                    S.op("dve", lambda e, t0=t0, n=n: e.scalar_tensor_tensor(
                        out=acc[:, :n], in0=pA[:, t0:t0 + n], scalar=gnw[:, l:l + 1], in1=rstd[:, :n], op0=ALU.mult,
                        op1=ALU.mult), reads=["pA", "rstd", "gnw", "acc"], writes=["acc"])
                    S.op("dve", lambda e, t0=t0, n=n, h=h: e.tensor_tensor(
                        out=mixT[:, h, t0:t0 + n], in0=acc[:, :n], in1=pD[:, t0:t0 + n], op=ALU.mult),
                        reads=["acc", "pD"], writes=[f"mix{h}", f"mix{h}s"])

        qsS = tmpw[:, 512:640].rearrange("p (h c) -> p h c", h=2)
        ksS = tmpw[:, 640:768].rearrange("p (h c) -> p h c", h=2)

        def moba_prompt(l):
            pA, pB = pbuf["A"], pbuf["B"]
            qb = av(WB + 1024 + 2 * 2112, 1056, BF16)
            kb = av(WB + 1024 + 2 * 2112 + 1056, 1056, BF16)
            Pb = [sq[i][:, 0:256].bitcast(BF16).rearrange("p (j q) -> p j q", j=4) for i in range(2)]
            for hp in range(2):
                proj_fm(l, 2824 + hp * 128, 128, pA, "pA")
                proj_fm(l, 3080 + hp * 128, 128, pB, "pB", eng="dve")
                S.op("act", lambda e: e.mul(out=qb[:, :], in_=pA[:, :], mul=0.125),
                     reads=["pA"], writes=["qb"])
                S.op("pool", lambda e, hp=hp: e.tensor_copy(out=qsS[:, hp, :], in_=pA[:, TP:T]), reads=["pA"], writes=["qsS"])
                S.op("pool", lambda e, hp=hp: e.tensor_copy(out=ksS[:, hp, :], in_=pB[:, TP:T]), reads=["pB"], writes=["ksS"])
                S.op("pool", lambda e: e.tensor_copy(out=kb[:, :], in_=pB[:, :]), reads=["pB"], writes=["kb"])
                S.op("dve", lambda e: e.tensor_reduce(out=kmT[:, :], in_=pB[:, 0:TP].rearrange("p (n j) -> p n j", n=8),
                                                      axis=AX.X, op=ALU.add), reads=["pB"], writes=["kmT"])
                def gmm(e):
                    for qt in range(16):
                        for hh in range(2):
                            pb = 64 * hh
                            r = e.matmul(PS[6][:, (qt * 2 + hh) * 8:(qt * 2 + hh) * 8 + 8],
                                         lhsT=pA[pb:pb + 64, qt * 128:(qt + 1) * 128], rhs=kmT[pb:pb + 64, :],
                                         start=True, stop=True)
                    return r
                S.op("pe", gmm, reads=["pA", "kmT"], writes=["ps6"])
                gm, top8, g01, selb = sg[0][:, 0:256], sg[0][:, 256:512], sg[1][:, 0:256], sg[1][:, 256:512]
                v4 = lambda t: t.rearrange("p (b r n) -> p b r n", b=8, r=4)
                c4 = lambda m: m[:, :, :].rearrange("p b (o n) -> p b o n", o=1).to_broadcast([128, 8, 4, 8])
                S.op("dve", lambda e: e.tensor_tensor(out=v4(gm), in0=v4(PS[6][:, 0:256]), in1=c4(negm), op=ALU.add),
                     reads=["ps6", "negm"], writes=["sg0"])
                for i in range(32):
                    S.op("dve", lambda e, i=i: e.max(out=top8[:, i * 8:(i + 1) * 8], in_=gm[:, i * 8:(i + 1) * 8]),
                         reads=["sg0"], writes=["sg0t"])
                S.op("dve", lambda e: e.tensor_tensor(
                    out=g01.rearrange("p (i n) -> p i n", n=8), in0=gm.rearrange("p (i n) -> p i n", n=8),
                    in1=top8.rearrange("p (i n) -> p i n", n=8)[:, :, 2:3].to_broadcast([128, 32, 8]), op=ALU.is_ge),
                    reads=["sg0", "sg0t"], writes=["sg1"])
                S.op("dve", lambda e: e.tensor_tensor(out=v4(g01), in0=v4(g01), in1=c4(pastm), op=ALU.mult),
                     reads=["sg1", "pastm"], writes=["sg1"])
                S.op("dve", lambda e: e.tensor_tensor(out=v4(g01), in0=v4(g01), in1=c4(ownm), op=ALU.add),
                     reads=["sg1", "ownm"], writes=["sg1"])
                S.op("dve", lambda e: e.tensor_scalar(out=selb, in0=g01, scalar1=-1.0, scalar2=1e30, op0=ALU.add,
                                                      op1=ALU.mult), reads=["sg1"], writes=["sg1b"])
                for qt in range(16):
                    selT = tmpw[0:8, (qt % 2) * 256:(qt % 2) * 256 + 256].rearrange("p (h q) -> p h q", h=2)
                    for hh in range(2):
                        i = qt * 2 + hh
                        S.op("pe", lambda e, i=i, hh=hh: e.transpose(out=PS[7][0:8, hh * 128:(hh + 1) * 128],
                                                                     in_=selb[:, i * 8:(i + 1) * 8], identity=ident[:]),
                             reads=["sg1b", "ident"], writes=["ps7"])
                        S.op("act", lambda e, selT=selT, hh=hh: e.copy(out=selT[:, hh, :],
                                                                       in_=PS[7][0:8, hh * 128:(hh + 1) * 128]),
                             reads=["ps7"], writes=[f"selT{qt % 2}_{hh}"])
                    for hh in range(2):
                        pb = 64 * hh
                        h = 2 * hp + hh
                        nkt = qt + 1
                        po, psm = PS[4], PS[5]
                        for c0 in range(0, nkt, 4):
                            nk = min(4, nkt - c0)
                            bi = (c0 // 4) % 2
                            pss = PS[bi]

                            def sm(e, c0=c0, nk=nk, pss=pss, pb=pb, qt=qt, selT=selT, hh=hh):
                                for j in range(nk):
                                    kt = c0 + j
                                    e.matmul(pss[:, j * 128:(j + 1) * 128], lhsT=kb[pb:pb + 64, kt * 128:(kt + 1) * 128],
                                             rhs=qb[pb:pb + 64, qt * 128:(qt + 1) * 128], start=True, stop=False)
                                    r = e.matmul(pss[:, j * 128:(j + 1) * 128], lhsT=E8[0:8, kt // 2, :],
                                                 rhs=selT[:, hh, :], start=False, stop=True)
                                return r
                            S.op("pe", sm, reads=["kb", "qb", "E8", f"selT{qt % 2}_{hh}"], writes=[f"ps{bi}"])
                            S.op("act", lambda e, nk=nk, pss=pss, bi=bi: e.activation(
                                out=Pb[bi][:, 0:nk, :], in_=pss[:, 0:nk * 128].rearrange("p (j q) -> p j q", j=nk),
                                func=AF.Exp), reads=[f"ps{bi}"], writes=[f"Pb{bi}"])
                            if c0 + nk == nkt:
                                S.op("pool", lambda e, nk=nk, bi=bi: e.tensor_tensor(
                                    out=Pb[bi][:, nk - 1, :], in0=Pb[bi][:, nk - 1, :], in1=triuf[:, :], op=ALU.mult),
                                    reads=[f"Pb{bi}", "triuf"], writes=[f"Pb{bi}"])

                            def pv(e, c0=c0, nk=nk, bi=bi, pb=pb, h=h, nkt=nkt):
                                for j in range(nk):
                                    kt = c0 + j
                                    e.matmul(po[pb:pb + 64, 0:128], lhsT=Vb[:, kt, h * 64:(h + 1) * 64], rhs=Pb[bi][:, j, :],
                                             start=(kt == 0), stop=(kt == nkt - 1))
                                    r = e.matmul(psm[pb:pb + 64, 0:128], lhsT=onesb[:, 0:64], rhs=Pb[bi][:, j, :],
                                                 start=(kt == 0), stop=(kt == nkt - 1))
                                return r
                            S.op("pe", pv, reads=["pD", f"Pb{bi}", "onesb"], writes=["ps4", "ps5"])
                        rs = rstd[pb:pb + 64, 0:128]
                        S.op("dve", lambda e, rs=rs, pb=pb: e.reciprocal(out=rs, in_=psm[pb:pb + 64, 0:128]),
                             reads=["ps5"], writes=[f"rs{hh}"])
                        S.op("dve", lambda e, rs=rs, pb=pb, qt=qt, hp=hp: e.tensor_tensor(
                            out=mixT[pb:pb + 64, 6 + hp, qt * 128:(qt + 1) * 128], in0=po[pb:pb + 64, 0:128], in1=rs,
                            op=ALU.mult), reads=["ps4", f"rs{hh}"], writes=[f"mix{6 + hp}"])


        def moba_sample(l):
            Kpg = av(WB + 1024, 4096, F32, "p (j f) -> p j f", j=16)
            Vpg = av(WB + 1024 + 2 * 2112, 4096, F32, "p (j f) -> p j f", j=16)
            KT = av(0, 2048, BF16, "p (h k) -> p h k", h=2)
            Vb16 = av(2048, 2080, BF16, "p (j h d) -> p j h d", j=16, h=4)
            wv = av(4224, 1024, BF16, "p (k f) -> p k f", k=KC)
            ck = cache_k.rearrange("l n p h d -> (l n p) (h d)")
            cv = cache_v.rearrange("l n p h d -> (l n p) (h d)")
            pix = pidx if l == 0 else pidx1
            S.dma("pool", lambda e: e.dma_start(out=wv, in_=w_in[l][:, 3336:3592].rearrange("(k p) f -> p k f", p=128)),
                  writes=["wv"])
            S.op("pool", lambda e: e.memset(Vb16[:, :, :, 64:65], 1.0), writes=["Vb16"])
            sm = acc
            Vn0 = sq[1][0:4, 144:274].bitcast(BF16).rearrange("p (h d) -> p h d", h=4)
            S.op("pool", lambda e: e.memset(Vn0[:, :, 64:65], 1.0), writes=["sq1"])
            smb = rstd.bitcast(BF16) if False else None
            for s_ in range(NSQ):
                for j in range(16):
                    S.dma("pool", lambda e, s_=s_, j=j: e.indirect_dma_start(
                        out=Kpg[:, j, :], out_offset=None, in_=ck,
                        in_offset=bass.IndirectOffsetOnAxis(ap=pix[:, s_ * 16 + j:s_ * 16 + j + 1], axis=0)),
                        reads=["pidx", "pidx1"], writes=["pA", "pB"])
                for j in range(16):
                    S.dma("pool", lambda e, s_=s_, j=j: e.indirect_dma_start(
                        out=Vpg[:, j, :], out_offset=None, in_=cv,
                        in_offset=bass.IndirectOffsetOnAxis(ap=pix[:, s_ * 16 + j:s_ * 16 + j + 1], axis=0)),
                        reads=["pidx", "pidx1"], writes=["pC", "pD"])
                for hp in range(2):
                    for q4 in range(4):
                        pb = (hp * 4 + q4) % 2

                        def trm(e, hp=hp, q4=q4, pb=pb):
                            for jj in range(4):
                                j = q4 * 4 + jj
                                r = e.transpose(out=PS[pb][:, jj * 128:(jj + 1) * 128], in_=Kpg[:, j, hp * 128:(hp + 1) * 128],
                                                identity=ident[:])
                            return r
                        S.op("pe", trm, reads=["pA", "pB", "ident"], writes=[f"ps{pb}"])
                        if pb == 0:
                            S.op("act", lambda e, hp=hp, q4=q4: e.copy(out=KT[:, hp, q4 * 512:(q4 + 1) * 512], in_=PS[0][:, :]),
                                 reads=["ps0"], writes=["KT"])
                        else:
                            S.op("dve", lambda e, hp=hp, q4=q4: e.tensor_copy(out=KT[:, hp, q4 * 512:(q4 + 1) * 512],
                                                                             in_=PS[1][:, :]), reads=["ps1"], writes=["KT"])

                def kms(e):
                    for hp in range(2):
                        for j in range(16):
                            r = e.matmul(PS[2][:, hp * 8 + j // 2:hp * 8 + j // 2 + 1], lhsT=Kpg[:, j, hp * 128:(hp + 1) * 128],
                                         rhs=ones[:, 0:1], start=(j % 2 == 0), stop=(j % 2 == 1))
                    return r
                S.op("pe", kms, reads=["pA", "pB", "ones"], writes=["ps2"])
                kmS = sm[:, 0:16].rearrange("p (h n) -> p h n", h=2)
                S.op("dve", lambda e: e.tensor_copy(out=sm[:, 0:16], in_=PS[2][:, 0:16]), reads=["ps2"], writes=["acc"])
                S.op("dve", lambda e: e.tensor_copy(out=Vb16[:, :, :, 0:64],
                                                    in_=Vpg[:, :, :].rearrange("p j (h d) -> p j h d", h=4)),
                     reads=["pC", "pD"], writes=["Vb16"])
                qsel = [qsS[:, hp, :].rearrange("p (t s) -> p s t", s=NSQ)[:, s_, :] for hp in range(2)]
                ksel = [ksS[:, hp, :].rearrange("p (t s) -> p s t", s=NSQ)[:, s_, :] for hp in range(2)]
                Qf = sm[:, 16:48].rearrange("p (h c) -> p h c", h=2)
                Qb = sq[0][:, 0:16].bitcast(BF16).rearrange("p (h c) -> p h c", h=2)
                Kn = sq[0][:, 16:20].bitcast(BF16).rearrange("p (h c) -> p h c", h=2)
                for hp in range(2):
                    S.op("dve", lambda e, hp=hp, qs_=qsel[hp]: e.tensor_tensor(
                        out=Qf[:, hp, :].rearrange("p (h t) -> p h t", h=4),
                        in0=qs_.rearrange("p (o t) -> p o t", o=1).to_broadcast([128, 4, 4]),
                        in1=bm4[:, hp, :].rearrange("p (h o) -> p h o", o=1).to_broadcast([128, 4, 4]), op=ALU.mult),
                        reads=["qsS", "bm4", "acc"], writes=["acc"])
                    S.op("act", lambda e, hp=hp, ks_=ksel[hp]: e.copy(out=Kn[:, hp, :], in_=ks_), reads=["ksS"], writes=["sq0"])
                S.op("act", lambda e: e.mul(out=Qb[:, :, :], in_=Qf[:, :, :], mul=0.125), reads=["acc"], writes=["sq0"])

                def gm(e):
                    e.matmul(PS[3][0:16, 0:8], lhsT=Qf[:, 0, :], rhs=kmS[:, 0, :], start=True, stop=False)
                    return e.matmul(PS[3][0:16, 0:8], lhsT=Qf[:, 1, :], rhs=kmS[:, 1, :], start=False, stop=True)
                S.op("pe", gm, reads=["acc"], writes=["ps3"])
                gt = sm[0:16, 64:128]
                S.op("dve", lambda e: e.tensor_copy(out=gt[:, 0:8], in_=PS[3][0:16, 0:8]), reads=["ps3"], writes=["acc"])
                S.op("dve", lambda e: e.max(out=gt[:, 8:16], in_=gt[:, 0:8]), reads=["acc"], writes=["acc"])
                S.op("dve", lambda e: e.tensor_scalar(out=gt[:, 16:24], in0=gt[:, 0:8], scalar1=gt[:, 10:11], scalar2=None,
                                                      op0=ALU.is_ge), reads=["acc"], writes=["acc"])
                S.op("dve", lambda e: e.tensor_scalar(out=gt[:, 24:32], in0=gt[:, 16:24], scalar1=-1.0, scalar2=1e30,
                                                      op0=ALU.add, op1=ALU.mult), reads=["acc"], writes=["acc"])
                S.op("pe", lambda e: e.transpose(out=PS[3][0:8, 16:32], in_=gt[:, 24:32], identity=ident[0:16, 0:16]),
                     reads=["acc", "ident"], writes=["ps3"])
                selT = sm[0:8, 128:144]
                S.op("act", lambda e: e.copy(out=selT, in_=PS[3][0:8, 16:32]), reads=["ps3"], writes=["acc"])

                def scm(e):
                    for kt in range(16):
                        o_ = PS[4][:, kt * 16:(kt + 1) * 16]
                        e.matmul(o_, lhsT=KT[:, 0, kt * 128:(kt + 1) * 128], rhs=Qb[:, 0, :], start=True, stop=False)
                        e.matmul(o_, lhsT=KT[:, 1, kt * 128:(kt + 1) * 128], rhs=Qb[:, 1, :], start=False, stop=False)
                        e.matmul(o_, lhsT=E8[0:8, kt // 2, :], rhs=selT, start=False, stop=True)
                    e.matmul(PS[5][0:4, 0:16], lhsT=Kn[:, 0, :], rhs=Qb[:, 0, :], start=True, stop=False)
                    return e.matmul(PS[5][0:4, 0:16], lhsT=Kn[:, 1, :], rhs=Qb[:, 1, :], start=False, stop=True)
                S.op("pe", scm, reads=["KT", "sq0", "E8", "acc"], writes=["ps4", "ps5"])
                Pb_ = sq[1][:, 0:128].bitcast(BF16)
                Pn = sq[1][0:4, 128:136].bitcast(BF16)
                S.op("act", lambda e: e.activation(out=Pb_, in_=PS[4][:, 0:256], func=AF.Exp), reads=["ps4"], writes=["sq1"])
                S.op("act", lambda e: e.activation(out=sm[0:4, 160:176], in_=PS[5][0:4, 0:16], func=AF.Exp), reads=["ps5"],
                     writes=["acc"])
                S.op("dve", lambda e: e.tensor_tensor(out=Pn.rearrange("p (h t) -> p h t", h=4),
                                                      in0=sm[0:4, 160:176].rearrange("p (h t) -> p h t", h=4),
                                                      in1=cm4[:, :, :], op=ALU.mult), reads=["acc", "cm4", "sq1"], writes=["sq1"])
                Vn = sq[1][0:4, 144:274].bitcast(BF16).rearrange("p (h d) -> p h d", h=4)

                def vnm(e, s_=s_):
                    for k in range(KC):
                        r = e.matmul(PS[5][0:4, 32:288], lhsT=xnT[:, k, TP:T].rearrange("p (t s) -> p s t", s=NSQ)[:, s_, :],
                                     rhs=wv[:, k, :], start=(k == 0), stop=(k == KC - 1))
                    return r
                S.op("pe", vnm, reads=["xn4", "wv"], writes=["ps5"])
                S.op("dve", lambda e: e.tensor_copy(out=Vn[:, :, 0:64], in_=PS[5][0:4, 32:288].rearrange("p (h d) -> p h d", h=4)),
                     reads=["ps5", "sq1"], writes=["sq1"])

                def pvm(e):
                    for h in range(4):
                        o_ = PS[6][0:4, h * 65:(h + 1) * 65]
                        for kt in range(16):
                            e.matmul(o_, lhsT=Pb_[:, kt * 16 + h * 4:kt * 16 + h * 4 + 4], rhs=Vb16[:, kt, h, :],
                                     start=(kt == 0), stop=False)
                        r = e.matmul(o_, lhsT=Pn[:, h * 4:(h + 1) * 4], rhs=Vn[:, h, :], start=False, stop=True)
                    return r
                S.op("pe", pvm, reads=["sq1", "Vb16"], writes=["ps6"])
                O3 = PS[6][0:4, 0:260].rearrange("p (h d) -> p h d", h=4)
                S.op("dve", lambda e: e.reciprocal(out=sm[0:4, 192:196], in_=O3[:, :, 64]), reads=["ps6"], writes=["acc"])
                S.op("dve", lambda e: e.tensor_tensor(
                    out=sm[0:4, 200:456].rearrange("p (h d) -> p h d", h=4), in0=O3[:, :, 0:64],
                    in1=sm[0:4, 192:196].rearrange("p (h o) -> p h o", o=1).to_broadcast([4, 4, 64]), op=ALU.mult),
                    reads=["ps6", "acc"], writes=["acc"])

                def otm(e):
                    for hp in range(2):
                        r = e.transpose(out=PS[7][:, hp * 4:(hp + 1) * 4], in_=sm[0:4, 200 + hp * 128:200 + (hp + 1) * 128],
                                        identity=ident[0:4, 0:4])
                    return r
                S.op("pe", otm, reads=["acc", "ident"], writes=["ps7"])
                S.op("act", lambda e, s_=s_: e.copy(
                    out=mixT[:, 6:8, TP:T].rearrange("p c (t s) -> p c s t", s=NSQ)[:, :, s_, :],
                    in_=PS[7][:, 0:8].rearrange("p (c t) -> p c t", c=2)), reads=["ps7"], writes=["mix6s", "mix7s"])

        def wout_apply(l):
            S.dma("pool", lambda e: e.dma_start(out=woutb, in_=w_out[l].rearrange("(k p) n -> p k n", p=128)),
                  writes=["pA", "pB"])
            mk = [f"mix{c}" for c in range(8)] + [f"mix{c}s" for c in range(8)]
            for g, (t0, n) in enumerate(GROUPS):
                for nn in range(KC):
                    pb = 4 + (nn % 2)
                    po = PS[pb]

                    def om(e, nn=nn, t0=t0, n=n, po=po):
                        for k in range(KC):
                            r = e.matmul(po[:, :n], lhsT=woutb[:, k, nn * 128:(nn + 1) * 128],
                                         rhs=mixT[:, k, t0:t0 + n], start=(k == 0), stop=(k == KC - 1))
                        return r
                    S.op("pe", om, reads=["pA", "pB"] + mk, writes=[f"ps{pb}"])
                    S.op("dve", lambda e, nn=nn, t0=t0, n=n, po=po: e.tensor_tensor(
                        out=xT[:, nn, t0:t0 + n], in0=po[:, :n], in1=xT[:, nn, t0:t0 + n], op=ALU.add),
                        reads=[f"ps{pb}", f"xT{g}"], writes=[f"xT{g}"])


        def tokmajor_rows(l):
            wkv = av(WB + 1024, 2048, BF16, "p (k f) -> p k f", k=KC)
            S.dma("pool", lambda e: e.dma_start(out=wkv, in_=w_in[l][:, 3080:3592].rearrange("(k p) f -> p k f", p=128)),
                  writes=["pA"])
            for tt in range(17):
                n = 128 if tt < 16 else 64
                b = tt % 2
                pp = PS[b]

                def km(e, tt=tt, n=n, pp=pp):
                    for k in range(KC):
                        r = e.matmul(pp[0:n, :], lhsT=xnT[:, k, tt * 128:tt * 128 + n], rhs=wkv[:, k, :],
                                     start=(k == 0), stop=(k == KC - 1))
                    return r
                S.op("pe", km, reads=["pA", f"xn{tt // 4}"], writes=[f"ps{b}"])
                if tt < 16:
                    S.op("pool" if False else "dve", lambda e, tt=tt, pp=pp: e.tensor_copy(out=Vb[:, tt, :], in_=pp[:, 256:512]),
                         reads=[f"ps{b}"], writes=["pD"])
                if b == 0:
                    S.op("act", lambda e, n=n, pp=pp: e.copy(out=sg[0][0:n, :], in_=pp[0:n, :]),
                         reads=["ps0"], writes=["sg0"])
                else:
                    S.op("dve", lambda e, n=n, pp=pp: e.tensor_copy(out=sg[1][0:n, :], in_=pp[0:n, :]),
                         reads=["ps1"], writes=["sg1"])
                if tt < 16:
                    S.dma("sp", lambda e, tt=tt, b=b: e.dma_start(out=p_k[l][tt * 128:(tt + 1) * 128, :],
                                                                 in_=sg[b][:, 0:256]), reads=[f"sg{b}"])
                    S.dma("act", lambda e, tt=tt, b=b: e.dma_start(out=p_v[l][tt * 128:(tt + 1) * 128, :],
                                                                  in_=sg[b][:, 256:512]), reads=[f"sg{b}"])
                else:
                    S.dma("sp", lambda e, b=b: e.dma_start(out=s_k[l].rearrange("s t f -> t s f"),
                                                          in_=sg[b][0:64, 0:256]), reads=[f"sg{b}"])
                    S.dma("act", lambda e, b=b: e.dma_start(out=s_v[l].rearrange("s t f -> t s f"),
                                                           in_=sg[b][0:64, 256:512]), reads=[f"sg{b}"])
            wq = av(WB + 1024, 6144, BF16, "p (k f) -> p k f", k=KC)
            S.dma("pool", lambda e: e.dma_start(out=wq, in_=w_in[l][:, 0:1536].rearrange("(k p) f -> p k f", p=128)),
                  writes=["pA", "pB", "pC"])
            crp = [xtok[0][:, 0:512], xtok[0][:, 512:1024], xtok[1][:, 0:512]]
            crk = ["xtok0a", "xtok0b", "xtok1a"]
            for (t0, n, key) in ((TP - 3, 3, "xn3"), (TP, 64, "xn4")):
                for j in range(3):
                    pp = PS[j % 2]

                    def cm(e, t0=t0, n=n, j=j, pp=pp):
                        for k in range(KC):
                            r = e.matmul(pp[0:n, :], lhsT=xnT[:, k, t0:t0 + n], rhs=wq[:, k, j * 512:(j + 1) * 512],
                                         start=(k == 0), stop=(k == KC - 1))
                        return r
                    S.op("pe", cm, reads=["pA", "pB", "pC", key], writes=[f"ps{j % 2}"])
                    S.op("act", lambda e, n=n, j=j, pp=pp: e.copy(out=crp[j][0:n, :], in_=pp[0:n, :]),
                         reads=[f"ps{j % 2}"], writes=[crk[j]])
                for j in range(3):
                    if n == 3:
                        S.dma("sp", lambda e, j=j: e.dma_start(out=p_conv[l][:, j * 512:(j + 1) * 512], in_=crp[j][0:3, :]),
                              reads=[crk[j]])
                    else:
                        S.dma("sp", lambda e, j=j: e.dma_start(
                            out=s_conv[l][:, :, j * 512:(j + 1) * 512].rearrange("s t f -> t s f"), in_=crp[j][16:64, :]),
                            reads=[crk[j]])

        def mixer(l):
            S.barrier()
            rms_norm_all(3 * l + 1)
            tokmajor_rows(l)
            for c in (6, 7):
                S.op("pool", lambda e, c=c: e.memset(mixT[:, c, :], 0.0), writes=[f"mix{c}", f"mix{c}s"])
            moba_prompt(l)
            S.barrier()
            if stage == 1:
                raise StopIteration
            moba_sample(l)
            S.barrier()
            if stage == 3:
                raise StopIteration
            gdn_mixer(l)
            S.barrier()
            if stage == 2:
                raise StopIteration
            sconv_mixer(l)
            S.barrier()
            wout_apply(l)
            S.barrier()

        try:
            for l in range(DEPTH):
                ffn(l, 0, 3 * l + 0)
                mixer(l)
                ffn(l, 1, 3 * l + 2)
        except StopIteration:
            pass

        S.barrier()
        if stage != 99 and stage < 0:
            dbg = dout("dbg", [128, 18944])
            S.dma("sp", lambda e: e.dma_start(out=dbg[:, :], in_=arena[:, :]))
        yn = av(0, 4096, F32, "p (k t) -> p k t", k=KC)
        for g, (t0, n) in enumerate(GROUPS if stage == 99 else []):
            rms_stats(g, t0, n)
            for k in range(KC):
                S.op("dve", lambda e, k=k, t0=t0, n=n: e.scalar_tensor_tensor(
                    out=yn[:, k, :n], in0=xT[:, k, t0:t0 + n], scalar=gains[:, 6, k:k + 1],
                    in1=rstd[:, :n], op0=ALU.mult, op1=ALU.mult),
                    reads=[f"xT{g}", "rstd", "gains"], writes=["yn"])
            for tl in range((n + 127) // 128):
                m = min(128, n - tl * 128)
                b = tl % 2
                for half in range(2):
                    pt = PS[6 + half]

                    def tr2(e, tl=tl, m=m, half=half, pt=pt):
                        for j in range(4):
                            k = half * 4 + j
                            r = e.transpose(out=pt[0:m, j * 128:(j + 1) * 128], in_=yn[:, k, tl * 128:tl * 128 + m],
                                            identity=ident[:])
                        return r
                    S.op("pe", tr2, reads=["yn", "ident"], writes=[f"ps{6 + half}"])
                    if half == 0:
                        S.op("dve", lambda e, m=m, b=b, pt=pt: e.tensor_copy(out=xtok[b][0:m, 0:512], in_=pt[0:m, :]),
                             reads=["ps6"], writes=[f"xtok{b}a"])
                    else:
                        S.op("act", lambda e, m=m, b=b, pt=pt: e.copy(out=xtok[b][0:m, 512:1024], in_=pt[0:m, :]),
                             reads=["ps7"], writes=[f"xtok{b}b"])
                S.dma("sp", lambda e, t0=t0, tl=tl, m=m, b=b: e.dma_start(
                    out=y[t0 + tl * 128:t0 + tl * 128 + m, :], in_=xtok[b][0:m, :]),
                    reads=[f"xtok{b}a", f"xtok{b}b"])
        S.emit()
        print("sched stats", S.stats, "sbuf bytes left", nc.sbuf_bytes_remaining)
    return nc


_W_KEYS = ["ffn1_w_gate", "ffn1_w_up", "ffn1_w_down", "ffn2_w_gate", "ffn2_w_up", "ffn2_w_down",
           "w_in", "w_out", "sc_conv_w", "gdn_conv_w", "gdn_a_log", "gdn_dt_bias", "gdn_norm_w"]


def kernel(**inp):
    nc = build_nc()
    norms = np.ascontiguousarray(np.concatenate([
        np.stack([inp["norm_ffn1"][l], inp["norm_mix"][l], inp["norm_ffn2"][l]]) for l in range(DEPTH)]
        + [inp["norm_final"][None, :]], axis=0).astype(np.float32))
    in_maps = []
    for c in range(NCORES):
        sl = slice(c * NSQ, (c + 1) * NSQ)
        xs = inp["x_sample"][sl].transpose(1, 0, 2).reshape(TS, D)
        xin = np.concatenate([inp["x_prompt"][c], xs], axis=0)
        m = {"xin": np.ascontiguousarray(xin), "norms": norms,
             "state_sconv": np.ascontiguousarray(inp["state_sconv"][:, sl]),
             "state_gdn": np.ascontiguousarray(inp["state_gdn"][:, sl]),
             "state_gdn_conv": np.ascontiguousarray(inp["state_gdn_conv"][:, sl]),
             "cache_k": inp["cache_k"], "cache_v": inp["cache_v"],
             "page_table": np.ascontiguousarray(inp["page_table"][sl].reshape(1, NSQ * 16).astype(np.int32))}
        for k in _W_KEYS:
            m[k] = inp[k]
        in_maps.append(m)
    res = run_bass_kernel_spmd(nc, in_maps, core_ids=list(range(NCORES)))
    R = res.results

    def cat_p(name, shp):
        return np.ascontiguousarray(np.stack([r[name].reshape((DEPTH,) + shp) for r in R], axis=1))

    def cat_s(name, shp):
        return np.ascontiguousarray(np.concatenate([r[name].reshape((DEPTH, NSQ) + shp) for r in R], axis=1))
    y_prompt = np.stack([r["y"][:TP] for r in R], axis=0)
    y_sample = np.concatenate([r["y"][TP:].reshape(4, NSQ, D).transpose(1, 0, 2) for r in R], axis=0)
    return (y_prompt, np.ascontiguousarray(y_sample),
            cat_p("p_gdn", (4, 128, 128)), cat_p("p_conv", (3, 1536)), cat_p("p_sconv", (2, 256)),
            cat_p("p_k", (TP, 4, 64)), cat_p("p_v", (TP, 4, 64)),
            cat_s("s_gdn", (4, 128, 128)), cat_s("s_conv", (3, 1536)), cat_s("s_sconv", (2, 256)),
            cat_s("s_k", (4, 4, 64)), cat_s("s_v", (4, 4, 64)))
```
